# Optimizing a Trainium2 kernel written in Bass

```python
import math
import jax
import jax.numpy as jnp
from jax import lax
import numpy as np

D_MODEL = 2048
BATCH = 4
SEQ = 2048
DEPTH = 2

CTX_LEN = 256
GRID_W = 64
EPS = 1e-6
N_BRANCH = 3

POOL_WINDOWS = (2, 4, 8, 16)
POOL_WIDTH = D_MODEL // 2
POOL_GROUP = POOL_WIDTH // len(POOL_WINDOWS)

DIFF_HEADS = 8
DIFF_HEAD_DIM = 64
DIFF_VDIM = 2 * DIFF_HEAD_DIM
DIFF_WIDTH = DIFF_HEADS * DIFF_VDIM
ROPE_THETA = 10000.0
Q_BLOCK = 128

GLA_HEADS = 4
GLA_DV = D_MODEL // 2 // GLA_HEADS
GLA_DK = GLA_DV // 2
GLA_KW = GLA_HEADS * GLA_DK
GLA_VW = GLA_HEADS * GLA_DV
GLA_RANK = 16
GLA_NORMALIZER = 16.0
GLA_CHUNK = 64

IN_SIZES = (POOL_WIDTH, POOL_WIDTH,
            DIFF_WIDTH, DIFF_WIDTH, DIFF_WIDTH, DIFF_WIDTH,
            GLA_KW, GLA_KW, GLA_VW, GLA_VW,
            GLA_RANK, GLA_RANK,
            N_BRANCH * D_MODEL)
D_IN = sum(IN_SIZES)

kernel_name = "hybrid_pool_diffattn_gla_prefix_block"


def split_in(z):
    idx = np.cumsum(IN_SIZES)[:-1].tolist()
    return jnp.split(z, idx, axis=-1)


def rms_norm(x, gain):
    xf = x.astype(jnp.float32)
    y = xf * lax.rsqrt(jnp.mean(xf * xf, axis=-1, keepdims=True) + EPS)
    return (y * gain.astype(jnp.float32)).astype(x.dtype)


def axial_rope(rows, cols):
    n_freq = DIFF_HEAD_DIM // 4
    inv = ROPE_THETA ** (-jnp.arange(n_freq, dtype=jnp.float32) / n_freq)
    ang = jnp.concatenate([rows.astype(jnp.float32)[:, None] * inv,
                           cols.astype(jnp.float32)[:, None] * inv], axis=-1)
    return jnp.cos(ang), jnp.sin(ang)


def apply_rope(t, cos, sin):
    half = DIFF_HEAD_DIM // 2
    tf = t.astype(jnp.float32)
    t1, t2 = tf[..., :half], tf[..., half:]
    cs = cos[None, :, None, None, :]
    sn = sin[None, :, None, None, :]
    return jnp.concatenate([t1 * cs - t2 * sn, t2 * cs + t1 * sn], axis=-1).astype(t.dtype)


def pool_mix(u, w_grp, scale):
    B, L, _ = u.shape
    uf = u.astype(jnp.float32)
    cs = jnp.concatenate([jnp.zeros_like(uf[:, :1]), jnp.cumsum(uf, axis=1)], axis=1)
    t = jnp.arange(L)
    groups = []
    for gi, w in enumerate(POOL_WINDOWS):
        sl = slice(gi * POOL_GROUP, (gi + 1) * POOL_GROUP)
        hi = jnp.clip(t + w // 2, 0, L)
        lo = jnp.clip(t - w // 2, 0, L)
        cnt = (hi - lo).astype(jnp.float32)[None, :, None]
        mean = (cs[:, hi, sl] - cs[:, lo, sl]) / cnt
        groups.append(mean - uf[:, :, sl])
    d = jnp.stack(groups, axis=2)
    y = jnp.einsum('blgc,gcd->blgd', d, w_grp.astype(jnp.float32)).reshape(B, L, POOL_WIDTH)
    return (y * scale.astype(jnp.float32)).astype(u.dtype)


def diff_attend(q, k, v, lam):
    s = jnp.einsum('bqhjd,bkhjd->bhjqk', q, k).astype(jnp.float32) * DIFF_HEAD_DIM ** -0.5
    p = jax.nn.softmax(s, axis=-1)
    a = p[:, :, 0] - lam * p[:, :, 1]
    return jnp.einsum('bhqk,bkhe->bqhe', a.astype(v.dtype), v)


def diff_post(o, subln, lambda_init):
    B, L = o.shape[:2]
    return (rms_norm(o, subln) * (1.0 - lambda_init)).reshape(B, L, DIFF_WIDTH)


def gla_log_decay(lr, w2, b):
    return jax.nn.log_sigmoid((lr @ w2 + b).astype(jnp.float32)) / GLA_NORMALIZER


def gla_chunk_scan(q, k, v, g, s0):
    B, L, H, _ = q.shape
    n = L // GLA_CHUNK
    C = GLA_CHUNK

    def chunks(t):
        return jnp.moveaxis(t.reshape(B, n, C, H, t.shape[-1]), 1, 0)

    lower = jnp.tril(jnp.ones((C, C), dtype=bool))[None, :, :, None, None]

    def step(s, inp):
        qc, kc, vc, gc = inp
        b = jnp.cumsum(gc, axis=1)
        o = jnp.einsum('bthk,bhkv->bthv', qc * jnp.exp(b), s)
        rel = jnp.exp(jnp.where(lower, b[:, :, None] - b[:, None], -jnp.inf))
        att = jnp.einsum('bthk,btshk,bshk->bhts', qc, rel, kc)
        o = o + jnp.einsum('bhts,bshv->bthv', att, vc)
        b_last = b[:, -1]
        s = s * jnp.exp(b_last)[..., None] + jnp.einsum(
            'bshk,bshv->bhkv', kc * jnp.exp(b_last[:, None] - b), vc)
        return s, o

    s, o = lax.scan(step, s0, (chunks(q), chunks(k), chunks(v), chunks(g)))
    return jnp.moveaxis(o, 0, 1).reshape(B, L, H, v.shape[-1]), s


def gla_final_state(k, v, g):
    b = jnp.cumsum(g, axis=1)
    return jnp.einsum('blhk,blhv->bhkv', k * jnp.exp(b[:, -1:] - b), v)


def gla_post(o, gain, gate):
    B, L = o.shape[:2]
    return rms_norm(o, gain).reshape(B, L, GLA_VW).astype(gate.dtype) * jax.nn.silu(gate)


def flip(t):
    return jnp.flip(t, axis=1)


def merge_branches(pool_o, diff_o, gla_o, mg, wbp, wbd, wbg, w_out):
    gp, gd, gg = jnp.split(jax.nn.sigmoid(mg), N_BRANCH, axis=-1)
    y = gp * (pool_o @ wbp) + gd * (diff_o @ wbd) + gg * (gla_o @ wbg)
    return y @ w_out


def hybrid_layer(x, ctx, c, c_ctx, cos, sin, norm_g, w_ada, b_ada, w_in, pool_w, pool_scale,
                 q_norm, k_norm, lam_q1, lam_k1, lam_q2, lam_k2, subln,
                 wgf, bgf, wgb, bgb, gla_norm, wbp, wbd, wbg, w_out,
                 lambda_init, need_ctx_out):
    B, L, _ = x.shape
    Lc = ctx.shape[1]
    shift, scale, gate = jnp.split(jax.nn.silu(c) @ w_ada + b_ada, 3, axis=-1)
    shift_c, scale_c, gate_c = jnp.split(jax.nn.silu(c_ctx) @ w_ada + b_ada, 3, axis=-1)
    h = rms_norm(x, norm_g) * (1.0 + scale[:, None]) + shift[:, None]
    hc = rms_norm(ctx, norm_g) * (1.0 + scale_c) + shift_c
    (pu, pg, dq, dk, dv, dg, gq, gk, gv, gg, glf, glb, mg) = split_in(h @ w_in)
    (pu_c, pg_c, dq_c, dk_c, dv_c, dg_c, gq_c, gk_c, gv_c, gg_c, glf_c, glb_c, mg_c) = split_in(hc @ w_in)

    pool_l = pool_mix(pu, pool_w, pool_scale) * jax.nn.silu(pg)

    def qk_heads(t, gain):
        return rms_norm(t.reshape(t.shape[0], t.shape[1], DIFF_HEADS, 2, DIFF_HEAD_DIM), gain)

    lam = (jnp.exp(jnp.sum(lam_q1.astype(jnp.float32) * lam_k1.astype(jnp.float32)))
           - jnp.exp(jnp.sum(lam_q2.astype(jnp.float32) * lam_k2.astype(jnp.float32)))
           + lambda_init)
    q_l = apply_rope(qk_heads(dq, q_norm), cos, sin)
    k_l = apply_rope(qk_heads(dk, k_norm), cos, sin)
    v_l = dv.reshape(B, L, DIFF_HEADS, DIFF_VDIM)
    k_c = qk_heads(dk_c, k_norm)
    v_c = dv_c.reshape(B, Lc, DIFF_HEADS, DIFF_VDIM)
    k_all = jnp.concatenate([k_c, k_l], axis=1)
    v_all = jnp.concatenate([v_c, v_l], axis=1)
    nb = L // Q_BLOCK
    qb = jnp.swapaxes(q_l.reshape(B, nb, Q_BLOCK, DIFF_HEADS, 2, DIFF_HEAD_DIM), 0, 1)
    o_l = lax.map(lambda qq: diff_attend(qq, k_all, v_all, lam), qb)
    o_l = jnp.swapaxes(o_l, 0, 1).reshape(B, L, DIFF_HEADS, DIFF_VDIM)
    diff_l = diff_post(o_l, subln, lambda_init) * jax.nn.silu(dg)

    def gla_inputs(tq, tk, tv, tlf, tlb):
        n = tq.shape[1]
        q_ = (tq.astype(jnp.float32) * GLA_DK ** -0.5).reshape(B, n, GLA_HEADS, GLA_DK)
        k_ = tk.astype(jnp.float32).reshape(B, n, GLA_HEADS, GLA_DK)
        v_ = tv.astype(jnp.float32).reshape(B, n, GLA_HEADS, GLA_DV)
        gf_ = gla_log_decay(tlf, wgf, bgf).reshape(B, n, GLA_HEADS, GLA_DK)
        gb_ = gla_log_decay(tlb, wgb, bgb).reshape(B, n, GLA_HEADS, GLA_DK)
        return q_, k_, v_, gf_, gb_

    ql, kl, vl, gfl, gbl = gla_inputs(gq, gk, gv, glf, glb)
    qc, kc, vc, gfc, gbc = gla_inputs(gq_c, gk_c, gv_c, glf_c, glb_c)
    s0 = jnp.zeros((B, GLA_HEADS, GLA_DK, GLA_DV), jnp.float32)
    if need_ctx_out:
        oc_f, s_f = gla_chunk_scan(qc, kc, vc, gfc, s0)
        oc_b, s_b = gla_chunk_scan(flip(qc), flip(kc), flip(vc), flip(gbc), s0)
        gla_c = gla_post(oc_f + flip(oc_b), gla_norm, gg_c)
    else:
        s_f = gla_final_state(kc, vc, gfc)
        s_b = gla_final_state(flip(kc), flip(vc), flip(gbc))
    ol_f, _ = gla_chunk_scan(ql, kl, vl, gfl, s_f)
    ol_b, _ = gla_chunk_scan(flip(ql), flip(kl), flip(vl), flip(gbl), s_b)
    gla_l = gla_post(ol_f + flip(ol_b), gla_norm, gg)

    x_new = x + gate[:, None] * merge_branches(pool_l, diff_l, gla_l, mg, wbp, wbd, wbg, w_out)
    if need_ctx_out:
        pool_c = pool_mix(pu_c, pool_w, pool_scale) * jax.nn.silu(pg_c)
        q_c = qk_heads(dq_c, q_norm)
        o_c = diff_attend(q_c, k_c, v_c, lam)
        diff_c = diff_post(o_c, subln, lambda_init) * jax.nn.silu(dg_c)
        ctx_new = ctx + gate_c * merge_branches(pool_c, diff_c, gla_c, mg_c, wbp, wbd, wbg, w_out)
    else:
        ctx_new = ctx
    return x_new, ctx_new


def setup_inputs(seed: int = 0) -> dict:
    key = jax.random.key(seed)
    ks = jax.random.split(key, 32)
    D = D_MODEL
    NL = DEPTH

    def nrm(k, shape, s):
        return jax.random.normal(k, shape, jnp.float32) * s

    return {
        "x": nrm(ks[0], (BATCH, SEQ, D), 1.0),
        "c": nrm(ks[1], (BATCH, D), 1.0),
        "ctx": nrm(ks[2], (BATCH, CTX_LEN, D), 1.0),
        "c_ctx": nrm(ks[3], (D,), 1.0),
        "norm_g": 1.0 + nrm(ks[4], (NL, D), 0.02),
        "w_ada": nrm(ks[5], (NL, D, 3 * D), 0.5 * D ** -0.5),
        "b_ada": nrm(ks[6], (NL, 3 * D), 0.01),
        "w_in": nrm(ks[7], (NL, D, D_IN), D ** -0.5),
        "pool_w": nrm(ks[8], (NL, len(POOL_WINDOWS), POOL_GROUP, POOL_GROUP), POOL_GROUP ** -0.5),
        "pool_scale": 1.0 + nrm(ks[9], (NL, POOL_WIDTH), 0.02),
        "diff_q_norm": 1.0 + nrm(ks[10], (NL, DIFF_HEAD_DIM), 0.02),
        "diff_k_norm": 1.0 + nrm(ks[11], (NL, DIFF_HEAD_DIM), 0.02),
        "diff_lam_q1": nrm(ks[12], (NL, DIFF_HEAD_DIM), 0.1),
        "diff_lam_k1": nrm(ks[13], (NL, DIFF_HEAD_DIM), 0.1),
        "diff_lam_q2": nrm(ks[14], (NL, DIFF_HEAD_DIM), 0.1),
        "diff_lam_k2": nrm(ks[15], (NL, DIFF_HEAD_DIM), 0.1),
        "diff_subln": 1.0 + nrm(ks[16], (NL, DIFF_VDIM), 0.02),
        "gla_w_gate_f": nrm(ks[17], (NL, GLA_RANK, GLA_KW), GLA_RANK ** -0.5),
        "gla_b_gate_f": nrm(ks[18], (NL, GLA_KW), 0.01),
        "gla_w_gate_b": nrm(ks[19], (NL, GLA_RANK, GLA_KW), GLA_RANK ** -0.5),
        "gla_b_gate_b": nrm(ks[20], (NL, GLA_KW), 0.01),
        "gla_norm": 1.0 + nrm(ks[21], (NL, GLA_DV), 0.02),
        "w_branch_pool": nrm(ks[22], (NL, POOL_WIDTH, D), POOL_WIDTH ** -0.5),
        "w_branch_diff": nrm(ks[23], (NL, DIFF_WIDTH, D), DIFF_WIDTH ** -0.5),
        "w_branch_gla": nrm(ks[24], (NL, GLA_VW, D), GLA_VW ** -0.5),
        "w_out": nrm(ks[25], (NL, D, D), D ** -0.5),
    }


def reference(x, c, ctx, c_ctx, norm_g, w_ada, b_ada, w_in, pool_w, pool_scale,
              diff_q_norm, diff_k_norm, diff_lam_q1, diff_lam_k1, diff_lam_q2, diff_lam_k2,
              diff_subln, gla_w_gate_f, gla_b_gate_f, gla_w_gate_b, gla_b_gate_b, gla_norm,
              w_branch_pool, w_branch_diff, w_branch_gla, w_out):
    L = x.shape[1]
    ROWS = L // GRID_W
    rows = jnp.repeat(jnp.arange(ROWS), GRID_W)
    cols = jnp.tile(jnp.arange(GRID_W), ROWS)
    cos, sin = axial_rope(rows, cols)
    for l in range(DEPTH):
        lambda_init = 0.8 - 0.6 * math.exp(-0.3 * l)
        x, ctx = hybrid_layer(
            x, ctx, c, c_ctx, cos, sin, norm_g[l], w_ada[l], b_ada[l], w_in[l],
            pool_w[l], pool_scale[l], diff_q_norm[l], diff_k_norm[l],
            diff_lam_q1[l], diff_lam_k1[l], diff_lam_q2[l], diff_lam_k2[l], diff_subln[l],
            gla_w_gate_f[l], gla_b_gate_f[l], gla_w_gate_b[l], gla_b_gate_b[l], gla_norm[l],
            w_branch_pool[l], w_branch_diff[l], w_branch_gla[l], w_out[l],
            lambda_init, l < DEPTH - 1)
    return x
```

```python
import math
from contextlib import ExitStack

import numpy as np
import concourse.bass as bass
import concourse.mybir as mybir
from concourse.bass_utils import run_bass_kernel_spmd

F32 = mybir.dt.float32
BF16 = mybir.dt.bfloat16
AF = mybir.ActivationFunctionType
ALU = mybir.AluOpType
AX = mybir.AxisListType

D = 2048
LX = 2048
LC = 256
NT = LX + LC
NTILE = NT // 128
EPS = 1e-6
D_IN = 15392
N_ACTIVE = 4


class Res:
    __slots__ = ("name", "w", "r", "dsem", "excl")

    def __init__(self, name, sched=None):
        self.name = name
        self.excl = False
        self.w = {}
        self.r = {}
        self.dsem = None
        if sched is not None:
            sched.all_res.append(self)


class Tile:
    __slots__ = ("t", "res")

    def __init__(self, t, res):
        self.t = t
        self.res = res


class Sched:
    def __init__(self, nc, stack):
        self.nc = nc
        self.stack = stack
        self.eng = {"pe": nc.tensor, "act": nc.scalar, "dve": nc.vector, "pool": nc.gpsimd, "sp": nc.sync}
        self.sem = {k: stack.enter_context(nc.semaphore("s_" + k)) for k in self.eng}
        self.cnt = {k: 0 for k in self.eng}
        self.seen = {k: {} for k in self.eng}
        self.pend = {k: ([], [], []) for k in self.eng}
        self.dtotal = {}
        self.free_dsems = {}
        self.dkind = {}
        self.all_res = []
        self.nsem = 5
        self.nwaits = 0
        self.ninst = 0
        self._junk = self.tile(stack, "junk_b", [128, 8], F32)

    def tile(self, stack, name, shape, dtype):
        self.ntile = getattr(self, "ntile", 0) + 1
        name = "t%d_%s" % (self.ntile, name)
        t = stack.enter_context(self.nc.sbuf_tensor(name, list(shape), dtype))
        return Tile(t, Res(name, self))

    def res(self, name):
        return Res(name, self)

    def _dsem(self, res, q="sp"):
        if res.dsem is None:
            kind = "sw" if q == "pool" else "hw"
            fl = self.free_dsems.setdefault(kind, [])
            if fl:
                res.dsem = fl.pop()
            else:
                res.dsem = self.stack.enter_context(self.nc.semaphore("d%d" % self.nsem))
                self.nsem += 1
                self.dtotal[res.dsem] = 0
                self.dkind[res.dsem] = kind
        return res.dsem

    def _wait_for(self, e, reads, writes, dwrites):
        need = {}
        for r in reads:
            for s, v in r.w.items():
                if need.get(s, 0) < v:
                    need[s] = v
        for w in writes:
            for s, v in w.w.items():
                if need.get(s, 0) < v:
                    need[s] = v
            for s, v in w.r.items():
                if need.get(s, 0) < v:
                    need[s] = v
        for w in dwrites:
            for s, v in w.r.items():
                if need.get(s, 0) < v:
                    need[s] = v
        own = self.sem[e]
        seen = self.seen[e]
        for s, v in need.items():
            if e == "pe" and s is own:
                continue
            if s in self.dtotal:
                v = max(v, self.dtotal[s])
            if seen.get(s, 0) >= v:
                continue
            self.eng[e].wait_ge(s, v)
            self.nwaits += 1
            seen[s] = v

    def _record(self, ev, reads, writes, dwrites):
        s, v = ev
        for r in reads:
            if r.r.get(s, 0) < v:
                r.r[s] = v
        for w in writes:
            w.w = {s: v}
            w.r = {}
        for w in dwrites:
            w.w[s] = v

    def op(self, e, fn, reads=(), writes=(), dwrites=(), inc=True):
        if any(r.excl for r in reads):
            writes = list(writes) + [r for r in reads if r.excl]
            reads = [r for r in reads if not r.excl]
        self._wait_for(e, reads, writes, dwrites)
        inst = fn(self.eng[e])
        self.ninst += 1
        if not inc:
            p = self.pend[e]
            p[0].extend(reads)
            p[1].extend(writes)
            p[2].extend(dwrites)
            return
        self.cnt[e] += 1
        inst.then_inc(self.sem[e], 1)
        ev = (self.sem[e], self.cnt[e])
        p = self.pend[e]
        self._record(ev, list(reads) + p[0], list(writes) + p[1], list(dwrites) + p[2])
        self.pend[e] = ([], [], [])

    def dma(self, q, out_ap, in_ap, sem_res, reads=(), writes=(), dwrites=()):
        self._wait_for(q, reads, writes, dwrites)
        inst = self.eng[q].dma_start(out=out_ap, in_=in_ap)
        self.ninst += 1
        s = self._dsem(sem_res, q)
        self.dtotal[s] += 16
        inst.then_inc(s, 16)
        self._record((s, self.dtotal[s]), reads, writes, dwrites)

    def barrier(self):
        for e in self.eng:
            assert not any(self.pend[e]), "pending ops at barrier on " + e
        e = "dve"
        seen = self.seen[e]
        for e2 in self.eng:
            s, v = self.sem[e2], self.cnt[e2]
            if v > seen.get(s, 0):
                self.eng[e].wait_ge(s, v)
                seen[s] = v
        for s, v in self.dtotal.items():
            if v > seen.get(s, 0):
                self.eng[e].wait_ge(s, v)
                seen[s] = v
        inst = self.eng[e].memset(self._junk.t[:, 0:1], 0.0)
        self.cnt[e] += 1
        inst.then_inc(self.sem[e], 1)
        v = self.cnt[e]
        seen[self.sem[e]] = v
        for e2 in self.eng:
            if e2 == e:
                continue
            self.eng[e2].wait_ge(self.sem[e], v)
            for s, vv in seen.items():
                if self.seen[e2].get(s, 0) < vv:
                    self.seen[e2][s] = vv
        for r in self.all_res:
            r.w = {}
            r.r = {}
            if r.dsem is not None:
                self.free_dsems.setdefault(self.dkind[r.dsem], []).append(r.dsem)
                r.dsem = None


POOL_WINDOWS = (2, 4, 8, 16)


def _band_tables():
    out = np.zeros((128, 4, 5, 128), np.float32)
    s = np.arange(128)[:, None]
    t = np.arange(128)[None, :]
    for gi, w in enumerate(POOL_WINDOWS):
        h = w // 2
        inwin = lambda ss: ((ss >= t - h) & (ss < t + h)).astype(np.float32)
        eye = (s == t).astype(np.float32)
        out[:, gi, 0] = inwin(s - 128) / w
        out[:, gi, 1] = inwin(s) / w - eye
        out[:, gi, 2] = inwin(s + 128) / w
        cnt_first = np.minimum(t + h, 10 ** 9) - np.maximum(t - h, 0)
        out[:, gi, 3] = inwin(s) / cnt_first - eye
        cnt_last = np.minimum(t + h, 128) - (t - h)
        out[:, gi, 4] = inwin(s) / cnt_last - eye
    return out.reshape(128, 20 * 128)


def _rope_tables():
    n_freq = 16
    inv = (10000.0 ** (-np.arange(n_freq, dtype=np.float32) / n_freq)).astype(np.float32)
    tt = np.arange(LX)
    rows = (tt // 64).astype(np.float32)
    cols = (tt % 64).astype(np.float32)
    ang = np.concatenate([rows[:, None] * inv, cols[:, None] * inv], axis=-1).astype(np.float32)
    cos = np.cos(ang).astype(np.float32).reshape(16, 128, 32).transpose(1, 0, 2)
    sin = np.sin(ang).astype(np.float32).reshape(16, 128, 32).transpose(1, 0, 2)
    return np.ascontiguousarray(np.concatenate([cos, sin], axis=1).reshape(128, 32 * 32))


def _tri_tables():
    s = np.arange(64)[:, None]
    t = np.arange(64)[None, :]
    mf = (s <= t).astype(np.float32)
    mb = (s >= t).astype(np.float32)
    out = np.zeros((64, 4, 64), np.float32)
    out[:, 0] = mf
    out[:, 1] = mb
    out[:, 2] = -mf / 16.0
    out[:, 3] = -mb / 16.0
    return out.reshape(64, 256)


def _tri2_tables():
    s = np.arange(128)[:, None]
    t = np.arange(128)[None, :]
    same = ((s // 64) == (t // 64)).astype(np.float32)
    out = np.zeros((128, 6, 128), np.float32)
    out[:, 0] = same * (s <= t)
    out[:, 1] = same * (s >= t)
    out[:, 2] = -(same * (s <= t)) / 16.0
    out[:, 3] = -(same * (s >= t)) / 16.0
    out[:, 4] = -(same * (s > t)) / 16.0
    out[:, 5] = -(same * (s < t)) / 16.0
    return out.reshape(128, 6 * 128)


_G = {}
_off = 0
for _n, _s in (("pu", 1024), ("pg", 1024), ("dq", 1024), ("dk", 1024), ("dv", 1024), ("dg", 1024),
               ("gq", 512), ("gk", 512), ("gv", 1024), ("gg", 1024), ("glf", 16), ("glb", 16), ("mg", 6144)):
    _G[_n] = (_off, _s)
    _off += _s
assert _off == D_IN

TOK_GROUPS = [(0, 512), (512, 512), (1024, 512), (1536, 512), (2048, 256)]


def build_program(n_layers=2, dbg=False, stop_after=None):
    nc = bass.Bass("TRN2", target_bir_lowering=False)

    def din(name, shape, dt=F32):
        return nc.dram_tensor(name, list(shape), dt, kind="ExternalInput").ap()

    scratch_kind = "ExternalOutput" if dbg else "Internal"

    def dscr(name, shape, dt):
        return nc.dram_tensor(name, list(shape), dt, kind=scratch_kind).ap()

    xin = din("xin", [NT, D])
    cT_d = din("cT", [128, 32])
    w_ada = din("w_ada", [2, D, 3 * D])
    w_in = din("w_in", [2, D, D_IN])
    pool_w = din("pool_w", [2, 4, 256, 256])
    wb_d = [din("w_branch_pool", [2, 1024, D]), din("w_branch_diff", [2, 1024, D]), din("w_branch_gla", [2, 1024, D])]
    w_out = din("w_out", [2, D, D])
    colp_d = din("colp", [128, 2 * 72])
    rowp_d = din("rowp", [2, 768])
    w2_d = din("w2", [2, 2, 17, 512])
    ident_d = din("ident", [128, 128])
    band_d = din("band", [128, 20 * 128])
    rope_d = din("rope", [128, 32 * 32])
    tri_d = din("tri", [64, 256])
    tri2_d = din("tri2", [128, 6 * 128])
    y_out = nc.dram_tensor("y", [LX, D], F32, kind="ExternalOutput").ap()

    U_d = dscr("U_d", [NT, 1024], BF16)
    SPG_d = dscr("SPG_d", [8, 128, NT], BF16)
    QT_d = dscr("QT_d", [8, 128, NT], BF16)
    KT_d = dscr("KT_d", [8, 128, NT], BF16)
    V_d = dscr("V_d", [NT, 1024], BF16)
    SDG_d = dscr("SDG_d", [NT, 1024], BF16)
    GQT_d = dscr("GQT_d", [4, 128, NT], F32)
    GKT_d = dscr("GKT_d", [4, 128, NT], F32)
    GK_d = dscr("GK_d", [NT, 512], F32)
    GV_d = dscr("GV_d", [NT, 1024], BF16)
    SGG_d = dscr("SGG_d", [NT, 1024], BF16)
    LRT_d = dscr("LRT_d", [2, 16, NT], F32)
    MGT_d = dscr("MGT_d", [48, 128, NT], BF16)
    POT_d = dscr("POT_d", [8, 128, NT], BF16)
    DOT_d = dscr("DOT_d", [8, 128, NT], BF16)
    GOT_d = dscr("GOT_d", [8, 128, NT], BF16)
    O_d = [dscr("OF_d", [NT, 1024], F32), dscr("OB_d", [NT, 1024], F32)]
    X1_d = dscr("X1_d", [NT, D], F32)
    YT_d = dscr("YT_d", [16, 128, NT], BF16)
    DOtm_d = dscr("DOtm_d", [NT, 1024], BF16)
    HT_dbg = dscr("HT_dbg", [16, 128, NT], BF16) if dbg else None
    MOD_dbg = dscr("MOD_dbg", [128, 192], F32) if dbg else None

    with ExitStack() as gs_:
        S = Sched(nc, gs_)
        psall = gs_.enter_context(nc.psum_tensor("psall", [128, 4096], F32))
        ps = [Tile(psall[:, i * 512:(i + 1) * 512], S.res("ps%d" % i)) for i in range(8)]
        for p_ in ps:
            p_.res.excl = True
        ps_rr = [0]

        def next_ps(lo=0, hi=8):
            i = ps_rr[0]
            ps_rr[0] = i + 1
            return ps[lo + i % (hi - lo)]

        dres = {n: S.res("dr_" + n) for n in ("U", "SPG", "QT", "KT", "V", "SDG", "GQT", "GKT", "GK", "GV", "SGG", "LRT",
                                               "MGT", "POT", "DOT", "GOT", "OF", "OB", "X1", "Y", "DBG", "YT", "DOTM")}

        G = gs_
        ident = S.tile(G, "ident", [128, 128], F32)
        identb = S.tile(G, "identb", [128, 128], BF16)
        ones_f = S.tile(G, "ones_f", [128, 128], F32)
        colp = S.tile(G, "colp", [128, 144], F32)
        rowp = S.tile(G, "rowp", [128, 2 * 768], F32)
        band = S.tile(G, "band", [128, 20, 128], BF16)
        rope = S.tile(G, "rope", [128, 32, 32], F32)
        tri = S.tile(G, "tri", [64, 4, 64], F32)
        trib = S.tile(G, "trib", [64, 2, 64], BF16)
        mod = S.tile(G, "mod", [128, 96, 2], F32)
        gsc = S.tile(G, "gsc", [128, 32, 2], F32)
        small = S.tile(G, "small", [128, 64], F32)

        S.dma("sp", ident.t[:], ident_d, ident.res, writes=[ident.res])
        S.dma("pool", identb.t[:], ident_d, identb.res, writes=[identb.res])
        S.dma("sp", colp.t[:], colp_d, colp.res, writes=[colp.res])
        S.dma("sp", rowp.t[:].rearrange("p (l n) -> p l n", l=2), rowp_d.partition_broadcast(128), rowp.res, writes=[rowp.res])
        S.dma("pool", band.t[:].rearrange("p a b -> p (a b)"), band_d, band.res, writes=[band.res])
        S.dma("sp", rope.t[:].rearrange("p a b -> p (a b)"), rope_d, rope.res, writes=[rope.res])
        S.dma("sp", tri.t[:].rearrange("p a b -> p (a b)"), tri_d, tri.res, writes=[tri.res])
        S.dma("pool", trib.t[:].rearrange("p a b -> p (a b)"), tri_d[:, 0:128], trib.res, writes=[trib.res])
        S.op("dve", lambda e: e.memset(ones_f.t[:], 1.0), writes=[ones_f.res])

        with ExitStack() as st:
            ct = S.tile(st, "ct", [128, 32], F32)
            sct = S.tile(st, "sct", [128, 32], BF16)
            wab = [S.tile(st, "wab%d" % i, [128, 16, 512], BF16) for i in range(2)]
            S.dma("sp", ct.t[:], cT_d, ct.res, writes=[ct.res])
            S.op("act", lambda e: e.activation(out=sct.t[:], in_=ct.t[:], func=AF.Silu), reads=[ct.res], writes=[sct.res])
            n = 0
            for l in range(n_layers):
                for j in range(12):
                    wb = wab[n % 2]
                    n += 1
                    S.dma("pool", wb.t[:], w_ada[l, :, j * 512:(j + 1) * 512].rearrange("(k p) n -> p k n", p=128),
                          wb.res, writes=[wb.res])
                    for m in range(4):
                        pt = next_ps()
                        for k in range(16):
                            S.op("pe", lambda e: e.matmul(pt.t[:, 0:2], lhsT=wb.t[:, k, m * 128:(m + 1) * 128],
                                                          rhs=sct.t[:, 2 * k:2 * k + 2], start=(k == 0), stop=(k == 15)),
                                 reads=[wb.res, sct.res], writes=[pt.res], inc=(k == 15))
                        col = j * 4 + m
                        S.op("act", lambda e: e.activation(out=mod.t[:, l * 48 + col, :], in_=pt.t[:, 0:2], func=AF.Identity,
                                                           bias=colp.t[:, l * 72 + 16 + col:l * 72 + 17 + col], scale=1.0),
                             reads=[pt.res, colp.res], dwrites=[mod.res])
            for l in range(n_layers):
                for r in range(2):
                    S.op("dve", lambda e: e.scalar_tensor_tensor(out=gsc.t[:, l * 16:(l + 1) * 16, r],
                                                                 in0=mod.t[:, l * 48 + 16:l * 48 + 32, r], scalar=1.0,
                                                                 in1=colp.t[:, l * 72:l * 72 + 16], op0=ALU.add, op1=ALU.mult),
                         reads=[mod.res, colp.res], dwrites=[gsc.res])
            if dbg:
                S.dma("sp", MOD_dbg, mod.t[:].rearrange("p a b -> p (a b)"), mod.res, reads=[mod.res], writes=[dres["DBG"]])
            S.barrier()
        if stop_after == "A":
            S.barrier()
            return nc

        for l in range(n_layers):
            need_ctx = l < n_layers - 1
            xsrc = xin if l == 0 else X1_d
            lam_init = 0.8 - 0.6 * math.exp(-0.3 * l)
            RP = l * 768

            with ExitStack() as lst:
                hT = S.tile(lst, "hT", [128, 16, NT], BF16)
                hres = [S.res("hT%d" % i) for i in range(NTILE)]

                wbuf = [S.tile(lst, "wbuf%d" % i, [128, 16, 512], BF16) for i in range(3)]
                preload = {}
                for pi_, c0_ in enumerate((0, 512)):
                    S.dma("pool", wbuf[pi_].t[:, :, 0:512], w_in[l, :, c0_:c0_ + 512].rearrange("(k p) n -> p k n", p=128),
                          wbuf[pi_].res, writes=[wbuf[pi_].res])
                    preload[(c0_, 512)] = wbuf[pi_]

                with ExitStack() as st:
                    BD = 3
                    xb = [S.tile(st, "xb%d" % i, [128, D], F32) for i in range(BD)]
                    junk = S.tile(st, "junk", [128, D], BF16)
                    ssb = [S.tile(st, "ssb%d" % i, [128, 4], F32) for i in range(BD)]

                    def b_tile(i):
                        r = 1 if i < 2 else 0
                        xt = xb[i % BD]
                        ss = ssb[i % BD]
                        S.dma("sp", xt.t[:], xsrc[i * 128:(i + 1) * 128, :], xt.res, reads=[dres["X1"]], writes=[xt.res])
                        S.op("act", lambda e: e.activation(out=junk.t[:], in_=xt.t[:], func=AF.Square, accum_out=ss.t[:, 0:1]),
                             reads=[xt.res], writes=[junk.res, ss.res])
                        yield
                        S.op("dve", lambda e: e.tensor_scalar(out=ss.t[:, 1:2], in0=ss.t[:, 0:1], scalar1=1.0 / D, scalar2=EPS,
                                                              op0=ALU.mult, op1=ALU.add), reads=[ss.res], writes=[ss.res])
                        yield
                        S.op("act", lambda e: e.activation(out=ss.t[:, 2:3], in_=ss.t[:, 1:2], func=AF.Sqrt), reads=[ss.res], writes=[ss.res])
                        yield
                        S.op("dve", lambda e: e.reciprocal(out=ss.t[:, 3:4], in_=ss.t[:, 2:3]), reads=[ss.res], writes=[ss.res])
                        S.op("dve", lambda e: e.tensor_scalar(out=xt.t[:], in0=xt.t[:], scalar1=ss.t[:, 3:4], scalar2=None, op0=ALU.mult),
                             reads=[xt.res, ss.res], writes=[xt.res])
                        yield
                        for kb in range(4):
                            pt = next_ps()
                            for kk in range(4):
                                k = kb * 4 + kk
                                S.op("pe", lambda e: e.transpose(out=pt.t[:, kk * 128:(kk + 1) * 128], in_=xt.t[:, k * 128:(k + 1) * 128],
                                                                 identity=ident.t[:]),
                                     reads=[xt.res, ident.res], writes=[pt.res], inc=(kk == 3))
                            yield
                            for kk in range(4):
                                k = kb * 4 + kk
                                o_ap = hT.t[:, k, i * 128:(i + 1) * 128]
                                i_ap = pt.t[:, kk * 128:(kk + 1) * 128]
                                sc = gsc.t[:, l * 16 + k, r:r + 1]
                                bi = mod.t[:, l * 48 + k, r:r + 1]
                                if kb % 2 == 0:
                                    S.op("act", lambda e: e.activation(out=o_ap, in_=i_ap, func=AF.Identity, bias=bi, scale=sc),
                                         reads=[pt.res, gsc.res, mod.res], dwrites=[hres[i]])
                                else:
                                    S.op("dve", lambda e: e.tensor_scalar(out=o_ap, in0=i_ap, scalar1=sc, scalar2=bi, op0=ALU.mult, op1=ALU.add),
                                         reads=[pt.res, gsc.res, mod.res], dwrites=[hres[i]])
                            yield

                    active = []
                    nxt = 0
                    rnd = 0
                    while nxt < NTILE or active:
                        if nxt < NTILE and len(active) < BD and rnd % 4 == 0:
                            active.append(b_tile(nxt))
                            nxt += 1
                        rnd += 1
                        for g_ in list(active):
                            try:
                                next(g_)
                            except StopIteration:
                                active.remove(g_)
                    if dbg and l == 0:
                        for k in range(16):
                            S.dma("sp", HT_dbg[k], hT.t[:, k, :], hT.res, reads=hres, dwrites=[dres["DBG"]])
                    S.barrier()
                if stop_after == "B":
                    return nc

                with ExitStack() as st:
                    wn = [2]

                    def load_w(c0, width):
                        if (c0, width) in preload:
                            return preload.pop((c0, width))
                        wb = wbuf[wn[0] % 3]
                        wn[0] += 1
                        S.dma("pool", wb.t[:, :, 0:width], w_in[l, :, c0:c0 + width].rearrange("(k p) n -> p k n", p=128),
                              wb.res, writes=[wb.res])
                        return wb

                    stg_b = [S.tile(st, "stgb%d" % i, [128, 512], BF16) for i in range(3)]
                    stg_f = [S.tile(st, "stgf%d" % i, [128, 512], F32) for i in range(3)]
                    sn = [0]

                    def nstg(fp32=False):
                        sn[0] += 1
                        return (stg_f if fp32 else stg_b)[sn[0] % 3]

                    def mm_tm(pt, i, wb, width):
                        for k in range(16):
                            S.op("pe", lambda e: e.matmul(pt.t[:, 0:width], lhsT=hT.t[:, k, i * 128:(i + 1) * 128], rhs=wb.t[:, k, 0:width],
                                                          start=(k == 0), stop=(k == 15)),
                                 reads=[hres[i], wb.res], writes=[pt.res], inc=(k == 15))

                    def mm_fm(pt, wb, m0, msz, t0, tsz):
                        tl = list(range(t0 // 128, (t0 + tsz) // 128))
                        for k in range(16):
                            S.op("pe", lambda e: e.matmul(pt.t[0:msz, 0:tsz], lhsT=wb.t[:, k, m0:m0 + msz], rhs=hT.t[:, k, t0:t0 + tsz],
                                                          start=(k == 0), stop=(k == 15)),
                                 reads=[hres[i] for i in tl] + [wb.res], writes=[pt.res], inc=(k == 15))

                    ecnt = [0]

                    def simple_tm(gname, dst, dname, func, fp32=False):
                        g0, gsz = _G[gname]
                        for c0 in range(0, gsz, 512):
                            wb = load_w(g0 + c0, 512)
                            for i in range(NTILE):
                                pt = next_ps()
                                mm_tm(pt, i, wb, 512)
                                sg = nstg(fp32)
                                ecnt[0] += 1
                                if func is not None:
                                    S.op("act", lambda e: e.activation(out=sg.t[:], in_=pt.t[:], func=func), reads=[pt.res], writes=[sg.res])
                                elif ecnt[0] % 2 == 0:
                                    S.op("act", lambda e: e.activation(out=sg.t[:], in_=pt.t[:], func=AF.Copy), reads=[pt.res], writes=[sg.res])
                                else:
                                    S.op("dve", lambda e: e.tensor_copy(out=sg.t[:], in_=pt.t[:]), reads=[pt.res], writes=[sg.res])
                                S.dma("sp", dst[i * 128:(i + 1) * 128, c0:c0 + 512], sg.t[:], sg.res, reads=[sg.res], dwrites=[dres[dname]])

                    def simple_fm(gname, dst, dname, func, scale=1.0, fp32=False, chunk0=0):
                        g0, gsz = _G[gname]
                        for c0 in range(0, gsz, 512):
                            wb = load_w(g0 + c0, 512)
                            for m in range(4):
                                ch = chunk0 + (c0 // 128) + m
                                for (t0, tsz) in TOK_GROUPS:
                                    pt = next_ps()
                                    mm_fm(pt, wb, m * 128, 128, t0, tsz)
                                    sg = nstg(fp32)
                                    ecnt[0] += 1
                                    if func is not None:
                                        S.op("act", lambda e: e.activation(out=sg.t[:, 0:tsz], in_=pt.t[:, 0:tsz], func=func, scale=scale),
                                             reads=[pt.res], writes=[sg.res])
                                    elif ecnt[0] % 2 == 0:
                                        S.op("act", lambda e: e.activation(out=sg.t[:, 0:tsz], in_=pt.t[:, 0:tsz], func=AF.Copy, scale=scale),
                                             reads=[pt.res], writes=[sg.res])
                                    else:
                                        S.op("dve", lambda e: e.tensor_scalar(out=sg.t[:, 0:tsz], in0=pt.t[:, 0:tsz], scalar1=scale, scalar2=None,
                                                                              op0=ALU.mult), reads=[pt.res], writes=[sg.res])
                                    S.dma("sp", dst[ch, :, t0:t0 + tsz], sg.t[:, 0:tsz], sg.res, reads=[sg.res], dwrites=[dres[dname]])

                    qk_stack = st
                    qk_tiles = {}

                    def qk_group(gname, dst, dname, gain_off):
                        g0, gsz = _G[gname]
                        DEPTH = 3
                        if True:
                            q = qk_stack
                            if not qk_tiles:
                                qk_tiles["sqs"] = [S.tile(q, "qk_sq%d" % i, [128, 512], F32) for i in range(DEPTH)]
                                qk_tiles["qns"] = [S.tile(q, "qk_qn%d" % i, [128, 512], F32) for i in range(DEPTH)]
                                qk_tiles["tas"] = [S.tile(q, "qk_ta%d" % i, [128, 8, 32], F32) for i in range(DEPTH)]
                                qk_tiles["tbs"] = [S.tile(q, "qk_tb%d" % i, [128, 8, 32], F32) for i in range(DEPTH)]
                                qk_tiles["tcs"] = [S.tile(q, "qk_tc%d" % i, [128, 8, 32], F32) for i in range(DEPTH)]
                                qk_tiles["tds"] = [S.tile(q, "qk_td%d" % i, [128, 8, 32], F32) for i in range(DEPTH)]
                                qk_tiles["qrs"] = [S.tile(q, "qk_qr%d" % i, [128, 512], BF16) for i in range(DEPTH)]
                                qk_tiles["st8s"] = [S.tile(q, "qk_s8%d" % i, [128, 16], F32) for i in range(DEPTH)]
                                qk_tiles["sTs"] = [S.tile(q, "qk_sT%d" % i, [128, 4, 128], BF16) for i in range(DEPTH)]
                            sqs, qns, tas, tbs, tcs, tds = (qk_tiles[k_] for k_ in ("sqs", "qns", "tas", "tbs", "tcs", "tds"))
                            qrs, st8s, sTs = (qk_tiles[k_] for k_ in ("qrs", "st8s", "sTs"))
                            gain = rowp.t[:, RP + gain_off:RP + gain_off + 64]
                            v3 = lambda ap: ap.rearrange("p (g d) -> p g d", g=8)

                            def qk_tile(c0, wb, i, n):
                                b_ = n % DEPTH
                                sq, qn, ta, tb, tc_, td, qo, st8, so = sqs[b_], qns[b_], tas[b_], tbs[b_], tcs[b_], tds[b_], qrs[b_], st8s[b_], sTs[b_]
                                pt = next_ps()
                                mm_tm(pt, i, wb, 512)
                                S.op("act", lambda e: e.activation(out=sq.t[:], in_=pt.t[:], func=AF.Square), reads=[pt.res], writes=[sq.res])
                                yield
                                S.op("dve", lambda e: e.reduce_sum(out=st8.t[:, 0:8], in_=v3(sq.t[:]), axis=AX.X), reads=[sq.res], writes=[st8.res])
                                S.op("dve", lambda e: e.tensor_scalar(out=st8.t[:, 0:8], in0=st8.t[:, 0:8], scalar1=1.0 / 64, scalar2=EPS,
                                                                      op0=ALU.mult, op1=ALU.add), reads=[st8.res], writes=[st8.res])
                                yield
                                S.op("act", lambda e: e.activation(out=st8.t[:, 0:8], in_=st8.t[:, 0:8], func=AF.Sqrt), reads=[st8.res], writes=[st8.res])
                                yield
                                S.op("dve", lambda e: e.reciprocal(out=st8.t[:, 8:16], in_=st8.t[:, 0:8]), reads=[st8.res], writes=[st8.res])
                                S.op("dve", lambda e: e.tensor_tensor(out=v3(qn.t[:]), in0=v3(pt.t[:]),
                                                                      in1=st8.t[:, 8:16].unsqueeze(2).to_broadcast([128, 8, 64]), op=ALU.mult),
                                     reads=[pt.res, st8.res], writes=[qn.res])
                                yield
                                if i < 2:
                                    S.op("pool", lambda e: e.tensor_tensor(out=v3(qo.t[:]), in0=v3(qn.t[:]),
                                                                           in1=gain.unsqueeze(1).to_broadcast([128, 8, 64]), op=ALU.mult),
                                         reads=[qn.res, rowp.res], writes=[qo.res])
                                    yield
                                else:
                                    S.op("pool", lambda e: e.tensor_tensor(out=v3(qn.t[:]), in0=v3(qn.t[:]),
                                                                           in1=gain.unsqueeze(1).to_broadcast([128, 8, 64]), op=ALU.mult),
                                         reads=[qn.res, rowp.res], writes=[qn.res])
                                    yield
                                    pi = i - 2
                                    cosb = rope.t[:, pi, :].unsqueeze(1).to_broadcast([128, 8, 32])
                                    sinb = rope.t[:, 16 + pi, :].unsqueeze(1).to_broadcast([128, 8, 32])
                                    q1 = v3(qn.t[:])[:, :, 0:32]
                                    q2 = v3(qn.t[:])[:, :, 32:64]
                                    o1 = v3(qo.t[:])[:, :, 0:32]
                                    o2 = v3(qo.t[:])[:, :, 32:64]
                                    S.op("dve", lambda e: e.tensor_tensor(out=ta.t[:], in0=q1, in1=cosb, op=ALU.mult), reads=[qn.res, rope.res], writes=[ta.res])
                                    S.op("pool", lambda e: e.tensor_tensor(out=tb.t[:], in0=q2, in1=sinb, op=ALU.mult), reads=[qn.res, rope.res], writes=[tb.res])
                                    S.op("dve", lambda e: e.tensor_tensor(out=tc_.t[:], in0=q2, in1=cosb, op=ALU.mult), reads=[qn.res, rope.res], writes=[tc_.res])
                                    S.op("pool", lambda e: e.tensor_tensor(out=td.t[:], in0=q1, in1=sinb, op=ALU.mult), reads=[qn.res, rope.res], writes=[td.res])
                                    yield
                                    S.op("dve", lambda e: e.tensor_tensor(out=o1, in0=ta.t[:], in1=tb.t[:], op=ALU.subtract),
                                         reads=[ta.res, tb.res], writes=[qo.res])
                                    S.op("dve", lambda e: e.tensor_tensor(out=o2, in0=tc_.t[:], in1=td.t[:], op=ALU.add),
                                         reads=[tc_.res, td.res], writes=[qo.res])
                                    yield
                                p2 = next_ps()
                                pb = p2.t[:].bitcast(BF16)
                                for hh in range(4):
                                    S.op("pe", lambda e: e.transpose(out=pb[:, hh * 128:(hh + 1) * 128], in_=qo.t[:, hh * 128:(hh + 1) * 128],
                                                                     identity=identb.t[:]),
                                         reads=[qo.res, identb.res], writes=[p2.res], inc=(hh == 3))
                                yield
                                S.op("act", lambda e: e.activation(out=so.t[:].rearrange("p a b -> p (a b)"), in_=pb[:, 0:512], func=AF.Copy),
                                     reads=[p2.res], writes=[so.res])
                                h0 = c0 // 128
                                S.dma("sp", dst[h0:h0 + 4, :, i * 128:(i + 1) * 128].rearrange("h p t -> p h t"), so.t[:], so.res,
                                      reads=[so.res], dwrites=[dres[dname]])

                            work = []
                            for c0 in range(0, gsz, 512):
                                for i in range(NTILE):
                                    work.append((c0, i))
                            wbs = {}
                            active = []
                            nxt = 0
                            rnd = 0
                            while nxt < len(work) or active:
                                if nxt < len(work) and len(active) < DEPTH and rnd % 3 == 0:
                                    c0, i = work[nxt]
                                    if c0 not in wbs:
                                        wbs[c0] = load_w(g0 + c0, 512)
                                    active.append(qk_tile(c0, wbs[c0], i, nxt))
                                    nxt += 1
                                rnd += 1
                                for g_ in list(active):
                                    try:
                                        next(g_)
                                    except StopIteration:
                                        active.remove(g_)

                    simple_tm("pu", U_d, "U", None)
                    simple_fm("pg", SPG_d, "SPG", AF.Silu)
                    qk_group("dq", QT_d, "QT", 0)
                    qk_group("dk", KT_d, "KT", 64)
                    simple_tm("dv", V_d, "V", None)
                    simple_tm("dg", SDG_d, "SDG", AF.Silu)
                    simple_fm("gq", GQT_d, "GQT", None, scale=128.0 ** -0.5, fp32=True)
                    simple_fm("gk", GKT_d, "GKT", None, fp32=True)
                    simple_tm("gk", GK_d, "GK", None, fp32=True)
                    simple_tm("gv", GV_d, "GV", None)
                    simple_tm("gg", SGG_d, "SGG", AF.Silu)
                    g0 = _G["glf"][0]
                    wb = load_w(g0, 32)
                    for dd in range(2):
                        for (t0, tsz) in TOK_GROUPS:
                            pt = next_ps()
                            mm_fm(pt, wb, dd * 16, 16, t0, tsz)
                            sg = nstg(True)
                            S.op("dve", lambda e: e.tensor_copy(out=sg.t[0:16, 0:tsz], in_=pt.t[0:16, 0:tsz]), reads=[pt.res], writes=[sg.res])
                            S.dma("sp", LRT_d[dd, :, t0:t0 + tsz], sg.t[0:16, 0:tsz], sg.res, reads=[sg.res], dwrites=[dres["LRT"]])
                    simple_fm("mg", MGT_d, "MGT", AF.Sigmoid)
                    S.barrier()
            if stop_after == "C":
                return nc

            with ExitStack() as st:
                Ut = S.tile(st, "Ut", [128, NTILE, 1024], BF16)
                pw = S.tile(st, "pw", [128, 8, 256], BF16)
                dT = [S.tile(st, "dT%d" % i, [128, 2, NT], BF16) for i in range(2)]
                spg = [S.tile(st, "spg%d" % i, [128, 512], BF16) for i in range(3)]
                po = [S.tile(st, "po%d" % i, [128, 512], BF16) for i in range(3)]
                for i0 in range(0, NTILE, 6):
                    S.dma("sp", Ut.t[:, i0:i0 + 6, :], U_d[i0 * 128:(i0 + 6) * 128, :].rearrange("(i p) c -> p i c", p=128), Ut.res,
                          reads=[dres["U"]], dwrites=[Ut.res])
                S.dma("pool", pw.t[:], pool_w[l].rearrange("g (cc p) d -> p (g cc) d", p=128), pw.res, writes=[pw.res])
                nn = 0
                for g in range(4):
                    dt_ = dT[g % 2]
                    for cc in range(2):
                        cb = g * 2 + cc
                        for (t0, tsz) in TOK_GROUPS:
                            pt = next_ps()
                            tiles = list(range(t0 // 128, (t0 + tsz) // 128))
                            for ti, i in enumerate(tiles):
                                first = i in (0, 2)
                                last = i in (1, NTILE - 1)
                                contrib = [(i, 3 if first else (4 if last else 1))]
                                if not first:
                                    contrib.append((i - 1, 0))
                                if not last:
                                    contrib.append((i + 1, 2))
                                for ci, (j, var) in enumerate(contrib):
                                    S.op("pe", lambda e: e.matmul(pt.t[:, ti * 128:(ti + 1) * 128], lhsT=Ut.t[:, j, cb * 128:(cb + 1) * 128],
                                                                  rhs=band.t[:, g * 5 + var, :], start=(ci == 0), stop=(ci == len(contrib) - 1)),
                                         reads=[Ut.res, band.res], writes=[pt.res], inc=(ti == len(tiles) - 1 and ci == len(contrib) - 1))
                            nn += 1
                            if nn % 2 == 0:
                                S.op("act", lambda e: e.activation(out=dt_.t[:, cc, t0:t0 + tsz], in_=pt.t[:, 0:tsz], func=AF.Copy),
                                     reads=[pt.res], dwrites=[dt_.res])
                            else:
                                S.op("dve", lambda e: e.tensor_copy(out=dt_.t[:, cc, t0:t0 + tsz], in_=pt.t[:, 0:tsz]),
                                     reads=[pt.res], dwrites=[dt_.res])
                    d2 = [(db, t0, tsz) for db in range(2) for (t0, tsz) in TOK_GROUPS]

                    def spg_load(m):
                        db, t0, tsz = d2[m]
                        S.dma("sp", spg[m % 3].t[:, 0:tsz], SPG_d[g * 2 + db, :, t0:t0 + tsz], spg[m % 3].res, reads=[dres["SPG"]], writes=[spg[m % 3].res])
                    spg_load(0)
                    spg_load(1)
                    for m, (db, t0, tsz) in enumerate(d2):
                        ch = g * 2 + db
                        if m + 2 < len(d2):
                            spg_load(m + 2)
                        pt = next_ps()
                        for cc in range(2):
                            S.op("pe", lambda e: e.matmul(pt.t[:, 0:tsz], lhsT=pw.t[:, g * 2 + cc, db * 128:(db + 1) * 128],
                                                          rhs=dt_.t[:, cc, t0:t0 + tsz], start=(cc == 0), stop=(cc == 1)),
                                 reads=[pw.res, dt_.res], writes=[pt.res], inc=(cc == 1))
                        sp_ = spg[m % 3]
                        po_ = po[m % 3]
                        S.op("dve", lambda e: e.scalar_tensor_tensor(out=po_.t[:, 0:tsz], in0=pt.t[:, 0:tsz],
                                                                     scalar=colp.t[:, l * 72 + 64 + ch:l * 72 + 65 + ch],
                                                                     in1=sp_.t[:, 0:tsz], op0=ALU.mult, op1=ALU.mult),
                             reads=[pt.res, colp.res, sp_.res], writes=[po_.res])
                        S.dma("sp", POT_d[ch, :, t0:t0 + tsz], po_.t[:, 0:tsz], po_.res, reads=[po_.res], dwrites=[dres["POT"]])
                    dt_.res.w = {}
                S.barrier()
            if stop_after == "D":
                return nc

            with ExitStack() as st:
                KT = [S.tile(st, "KT%d" % i, [128, NT], BF16) for i in range(2)]
                QT = [S.tile(st, "QT%d" % i, [128, NT], BF16) for i in range(2)]
                Vx = [S.tile(st, "Vx%d" % i, [128, NTILE, 130], BF16) for i in range(2)]
                ptl = [S.tile(st, "ptl%d" % i, [128, 512], BF16) for i in range(3)]
                lam = S.tile(st, "lam", [128, 8], F32)
                lt = S.tile(st, "lamt", [128, 64], F32)
                subs = S.tile(st, "subs", [128, 128], F32)
                accs = [S.tile(st, "accs%d" % i, [128, 4, 2, 130], F32) for i in range(2)]
                rz = [S.tile(st, "rz%d" % i, [128, 4, 8], F32) for i in range(2)]
                t0_ = [S.tile(st, "t0_%d" % i, [128, 4, 128], F32) for i in range(2)]
                t1_ = [S.tile(st, "t1_%d" % i, [128, 4, 128], F32) for i in range(2)]
                sqt2 = [S.tile(st, "esq%d" % i, [128, 4, 128], F32) for i in range(2)]
                ob = [S.tile(st, "ob%d" % i, [128, 4, 128], BF16) for i in range(2)]
                sdg = [S.tile(st, "sdg%d" % i, [128, 4, 128], BF16) for i in range(2)]
                oT = [S.tile(st, "oT%d" % i, [128, 512], BF16) for i in range(2)]
                for a in range(2):
                    S.op("dve", lambda e: e.tensor_tensor(out=lt.t[:], in0=rowp.t[:, RP + 512 + a * 128:RP + 576 + a * 128],
                                                          in1=rowp.t[:, RP + 576 + a * 128:RP + 640 + a * 128], op=ALU.mult),
                         reads=[rowp.res], writes=[lt.res])
                    S.op("dve", lambda e: e.reduce_sum(out=lam.t[:, a:a + 1], in_=lt.t[:], axis=AX.X), reads=[lt.res], writes=[lam.res])
                S.op("act", lambda e: e.activation(out=lam.t[:, 2:4], in_=lam.t[:, 0:2], func=AF.Exp), reads=[lam.res], writes=[lam.res])
                S.op("dve", lambda e: e.tensor_tensor(out=lam.t[:, 4:5], in0=lam.t[:, 3:4], in1=lam.t[:, 2:3], op=ALU.subtract),
                     reads=[lam.res], writes=[lam.res])
                S.op("dve", lambda e: e.tensor_scalar(out=lam.t[:, 5:6], in0=lam.t[:, 4:5], scalar1=-lam_init, scalar2=None, op0=ALU.add),
                     reads=[lam.res], writes=[lam.res])
                S.op("act", lambda e: e.activation(out=subs.t[:], in_=rowp.t[:, RP + 128:RP + 256], func=AF.Copy, scale=1.0 - lam_init),
                     reads=[rowp.res], writes=[subs.res])
                for hb in range(2):
                    S.op("dve", lambda e: e.memset(Vx[hb].t[:, :, 128:130], 1.0), writes=[Vx[hb].res])

                accb = ps[0:4]
                accv = psall[:, 0:2048].rearrange("p (q j c) -> p q j c", q=4, j=2)
                ptl2 = [S.tile(st, "ptl2_%d" % i, [128, 2, 512], BF16) for i in range(2)]
                its = []
                for h in range(8):
                    qgroups = [(256 + 512 * a, 512, list(range(NTILE))) for a in range(4)]
                    if need_ctx:
                        qgroups = [(0, 256, [0, 1])] + qgroups
                    for (q0, qsz, ktiles) in qgroups:
                        for kti, kt in enumerate(ktiles):
                            its.append(dict(h=h, q0=q0, qsz=qsz, kt=kt, first=(kti == 0), last=(kti == len(ktiles) - 1)))
                loaded = set()
                hstart = {}
                fin = [0]

                def load_head(h):
                    loaded.add(h)
                    kt_, qt_, vx_ = KT[h % 2], QT[h % 2], Vx[h % 2]
                    S.dma("sp", kt_.t[:], KT_d[h], kt_.res, reads=[dres["KT"]], writes=[kt_.res])
                    S.dma("sp", qt_.t[:], QT_d[h], qt_.res, reads=[dres["QT"]], writes=[qt_.res])
                    S.dma("sp", vx_.t[:, :, 0:128], V_d[:, h * 128:(h + 1) * 128].rearrange("(i p) e -> p i e", p=128), vx_.res,
                          reads=[dres["V"]], dwrites=[vx_.res])

                def emit_S(n):
                    it = its[n]
                    h = it["h"]
                    kt_, qt_, vx_ = KT[h % 2], QT[h % 2], Vx[h % 2]
                    if h not in hstart:
                        hstart[h] = n
                    if h not in loaded:
                        load_head(h)
                    if h + 1 < 8 and (h + 1) not in loaded and n - hstart[h] >= 6:
                        load_head(h + 1)
                    kt, q0, qsz = it["kt"], it["q0"], it["qsz"]
                    sb = n % 2
                    pa, pb_ = ps[4 + 2 * sb], ps[5 + 2 * sb]
                    pt_ = ptl2[sb]
                    for j, p_s in enumerate((pa, pb_)):
                        S.op("pe", lambda e: e.matmul(p_s.t[:, 0:qsz], lhsT=kt_.t[j * 64:(j + 1) * 64, kt * 128:(kt + 1) * 128],
                                                      rhs=qt_.t[j * 64:(j + 1) * 64, q0:q0 + qsz], start=True, stop=True),
                             reads=[kt_.res, qt_.res], writes=[p_s.res], inc=(j == 1))
                    sview = psall[:, 2048 + sb * 1024:2048 + (sb + 1) * 1024].rearrange("p (j c) -> p j c", j=2)
                    S.op("act", lambda e: e.activation(out=pt_.t[:, :, 0:qsz], in_=sview[:, :, 0:qsz], func=AF.Exp, scale=0.125),
                         reads=[pa.res, pb_.res], writes=[pt_.res])

                def emit_PV(n):
                    it = its[n]
                    h = it["h"]
                    vx_ = Vx[h % 2]
                    kt, q0, qsz = it["kt"], it["q0"], it["qsz"]
                    nq = qsz // 128
                    pt_ = ptl2[n % 2]
                    for j in range(2):
                        for qi in range(nq):
                            S.op("pe", lambda e: e.matmul(accb[qi].t[:, j * 256:j * 256 + 130], lhsT=pt_.t[:, j, qi * 128:(qi + 1) * 128],
                                                          rhs=vx_.t[:, kt, :], start=(it["first"] and j == 0), stop=(it["last"] and j == 1),
                                                          skip_group_check=True),
                                 reads=[pt_.res, vx_.res], writes=[accb[qi].res], inc=(j == 1 and qi == nq - 1))
                    if not it["last"]:
                        return
                    f = fin[0] % 2
                    fin[0] += 1
                    ac, rz_, a0_, a1_, ob_, sdg_ = accs[f], rz[f], t0_[f], t1_[f], ob[f], sdg[f]
                    sqt = sqt2[f]
                    bc = lambda ap: ap.unsqueeze(2).to_broadcast([128, nq, 128])
                    S.dma("sp", sdg_.t[:, 0:nq, :], SDG_d[q0:q0 + qsz, h * 128:(h + 1) * 128].rearrange("(i p) e -> p i e", p=128), sdg_.res,
                          reads=[dres["SDG"]], writes=[sdg_.res])
                    S.op("dve", lambda e: e.tensor_copy(out=ac.t[:, 0:nq], in_=accv[:, 0:nq, :, 0:130]),
                         reads=[accb[qi].res for qi in range(nq)], writes=[ac.res])
                    S.op("dve", lambda e: e.reciprocal(out=rz_.t[:, 0:nq, 0:2], in_=ac.t[:, 0:nq, :, 128]), reads=[ac.res], writes=[rz_.res])
                    S.op("dve", lambda e: e.tensor_scalar(out=rz_.t[:, 0:nq, 1], in0=rz_.t[:, 0:nq, 1], scalar1=lam.t[:, 5:6], scalar2=None, op0=ALU.mult),
                         reads=[rz_.res, lam.res], writes=[rz_.res])
                    S.op("pool", lambda e: e.tensor_tensor(out=a1_.t[:, 0:nq], in0=ac.t[:, 0:nq, 1, 0:128], in1=bc(rz_.t[:, 0:nq, 1]), op=ALU.mult),
                         reads=[ac.res, rz_.res], writes=[a1_.res])
                    S.op("dve", lambda e: e.tensor_tensor(out=a0_.t[:, 0:nq], in0=ac.t[:, 0:nq, 0, 0:128], in1=bc(rz_.t[:, 0:nq, 0]), op=ALU.mult),
                         reads=[ac.res, rz_.res], writes=[a0_.res])
                    S.op("pool", lambda e: e.tensor_tensor(out=a0_.t[:, 0:nq], in0=a0_.t[:, 0:nq], in1=a1_.t[:, 0:nq], op=ALU.add),
                         reads=[a0_.res, a1_.res], writes=[a0_.res])
                    S.op("pool", lambda e: e.tensor_tensor(out=sqt.t[:, 0:nq], in0=a0_.t[:, 0:nq], in1=a0_.t[:, 0:nq], op=ALU.mult),
                         reads=[a0_.res], writes=[sqt.res])

                    def fin_b():
                        S.op("dve", lambda e: e.reduce_sum(out=rz_.t[:, 0:nq, 2], in_=sqt.t[:, 0:nq], axis=AX.X), reads=[sqt.res], writes=[rz_.res])
                        S.op("dve", lambda e: e.tensor_scalar(out=rz_.t[:, 0:nq, 3], in0=rz_.t[:, 0:nq, 2], scalar1=1.0 / 128, scalar2=EPS,
                                                              op0=ALU.mult, op1=ALU.add), reads=[rz_.res], writes=[rz_.res])
                        S.op("act", lambda e: e.activation(out=rz_.t[:, 0:nq, 4], in_=rz_.t[:, 0:nq, 3], func=AF.Ln), reads=[rz_.res], writes=[rz_.res])
                        S.op("act", lambda e: e.activation(out=rz_.t[:, 0:nq, 5], in_=rz_.t[:, 0:nq, 4], func=AF.Exp, scale=-0.5), reads=[rz_.res], writes=[rz_.res])

                    def fin_c():
                        S.op("dve", lambda e: e.tensor_tensor(out=a0_.t[:, 0:nq], in0=a0_.t[:, 0:nq], in1=bc(rz_.t[:, 0:nq, 5]), op=ALU.mult),
                             reads=[a0_.res, rz_.res], writes=[a0_.res])
                        S.op("pool", lambda e: e.tensor_tensor(out=a0_.t[:, 0:nq], in0=a0_.t[:, 0:nq],
                                                               in1=subs.t[:].unsqueeze(1).to_broadcast([128, nq, 128]), op=ALU.mult),
                             reads=[a0_.res, subs.res], writes=[a0_.res])
                        S.op("pool", lambda e: e.tensor_tensor(out=ob_.t[:, 0:nq], in0=a0_.t[:, 0:nq], in1=sdg_.t[:, 0:nq], op=ALU.mult),
                             reads=[a0_.res, sdg_.res], writes=[ob_.res])
                        S.dma("sp", DOtm_d[q0:q0 + qsz, h * 128:(h + 1) * 128].rearrange("(i p) e -> p i e", p=128), ob_.t[:, 0:nq, :], ob_.res,
                              reads=[ob_.res], dwrites=[dres["DOTM"]])
                    deferred.append((n + 4, fin_b))
                    deferred.append((n + 7, fin_c))

                deferred = []
                LOOK = 1
                for n in range(len(its) + LOOK):
                    if n < len(its):
                        emit_S(n)
                    if n >= LOOK:
                        emit_PV(n - LOOK)
                    while deferred and deferred[0][0] <= n - LOOK:
                        deferred.pop(0)[1]()
                while deferred:
                    deferred.pop(0)[1]()
                S.barrier()
            with ExitStack() as st:
                dtm = [S.tile(st, "dtm%d" % i, [128, 1024], BF16) for i in range(3)]
                dTt = [S.tile(st, "dTt%d" % i, [128, 8, 128], BF16) for i in range(3)]
                e2_tiles = list(range(0 if need_ctx else 2, NTILE))

                def e2_load(m):
                    i = e2_tiles[m]
                    S.dma("sp", dtm[m % 3].t[:], DOtm_d[i * 128:(i + 1) * 128, :], dtm[m % 3].res, reads=[dres["DOTM"]], writes=[dtm[m % 3].res])
                e2_load(0)
                e2_load(1)
                for m, i in enumerate(e2_tiles):
                    if m + 2 < len(e2_tiles):
                        e2_load(m + 2)
                    d_, dT_ = dtm[m % 3], dTt[m % 3]
                    p2 = next_ps()
                    pb = p2.t[:].bitcast(BF16)
                    for cb in range(8):
                        S.op("pe", lambda e: e.transpose(out=pb[:, cb * 128:(cb + 1) * 128], in_=d_.t[:, cb * 128:(cb + 1) * 128], identity=identb.t[:]),
                             reads=[d_.res, identb.res], writes=[p2.res], inc=(cb == 7))
                    S.op("act", lambda e: e.activation(out=dT_.t[:].rearrange("p a b -> p (a b)"), in_=pb[:, 0:1024], func=AF.Copy),
                         reads=[p2.res], writes=[dT_.res])
                    S.dma("sp", DOT_d[:, :, i * 128:(i + 1) * 128].rearrange("c p t -> p c t"), dT_.t[:], dT_.res, reads=[dT_.res], dwrites=[dres["DOT"]])
                S.barrier()
            if stop_after == "E":
                return nc

            with ExitStack() as st:
                w2 = S.tile(st, "w2", [17, 2, 512], F32)
                Sf = [[S.tile(st, "Sf%d%d" % (d, hh), [128, 256], F32) for hh in range(4)] for d in range(2)]
                tri2 = S.tile(st, "tri2", [128, 6, 128], F32)
                S.dma("sp", tri2.t[:].rearrange("p a b -> p (a b)"), tri2_d, tri2.res, writes=[tri2.res])
                S.dma("sp", w2.t[:], w2_d[l].rearrange("d k n -> k d n"), w2.res, writes=[w2.res])
                for d in range(2):
                    for hh in range(4):
                        S.op("dve", lambda e: e.memset(Sf[d][hh].t[:], 0.0), writes=[Sf[d][hh].res])
                NB = 2

                def mk(name, shape, dt, n=NB):
                    return [[S.tile(st, "%s%d_%d" % (name, d, i), shape, dt) for i in range(n)] for d in range(2)]
                lr = mk("lr", [32, 128], F32)
                gk = mk("gk", [128, 512], F32)
                gv = mk("gv", [128, 1024], BF16)
                gqT = mk("gqT", [128, 4, 128], F32)
                gkT = mk("gkT", [128, 4, 128], F32)
                e1 = mk("e1", [128, 512], F32, 1)
                g1 = mk("g1", [128, 512], F32)
                eblt = mk("eblt", [128, 512], F32, 1)
                kps = mk("kps", [128, 512], BF16)
                eb = mk("eb", [128, 512], F32)
                enT = mk("enT", [128, 512], F32, 1)
                qpT = mk("qpT", [128, 4, 128], BF16)
                kpT = mk("kpT", [128, 4, 128], BF16)
                qpad = mk("qpad", [128, 4, 2, 128], BF16)
                atb = mk("atb", [128, 4, 128], BF16)
                osb = mk("osb", [128, 1024], F32)
                Sb2 = [[[S.tile(st, "Sb%d%d_%d" % (d, hh, v), [128, 256], BF16) for v in range(2)] for hh in range(4)] for d in range(2)]
                sver = [[0] * 4 for _ in range(2)]
                for d in range(2):
                    for i in range(NB):
                        S.op("dve", lambda e: e.memset(lr[d][i].t[:], 1.0), writes=[lr[d][i].res])
                        S.op("pool", lambda e: e.memset(qpad[d][i].t[:], 0.0), writes=[qpad[d][i].res])
                    for hh in range(4):
                        for v in range(2):
                            S.op("pool", lambda e: e.memset(Sb2[d][hh][v].t[:], 0.0), writes=[Sb2[d][hh][v].res])
                fwd = list(range(NTILE))
                bwd = [1, 0] + list(range(NTILE - 1, 1, -1))
                pc = [0]

                def gps():
                    pc[0] += 1
                    return ps[pc[0] % 8]

                def gla_load(d, i, n):
                    b = n % NB
                    t0 = i * 128
                    lr_, gk_, gv_, gqT_, gkT_ = lr[d][b], gk[d][b], gv[d][b], gqT[d][b], gkT[d][b]
                    S.dma("sp", lr_.t[0:16, :], LRT_d[d, :, t0:t0 + 128], lr_.res, reads=[dres["LRT"]], writes=[lr_.res])
                    S.dma("sp", gk_.t[:], GK_d[t0:t0 + 128, :], gk_.res, reads=[dres["GK"]], writes=[gk_.res])
                    S.dma("sp", gv_.t[:], GV_d[t0:t0 + 128, :], gv_.res, reads=[dres["GV"]], writes=[gv_.res])
                    S.dma("sp", gqT_.t[:], GQT_d[:, :, t0:t0 + 128].rearrange("h p t -> p h t"), gqT_.res, reads=[dres["GQT"]], writes=[gqT_.res])
                    S.dma("sp", gkT_.t[:], GKT_d[:, :, t0:t0 + 128].rearrange("h p t -> p h t"), gkT_.res, reads=[dres["GKT"]], writes=[gkT_.res])

                def gla_tile(d, i, n):
                    b = n % NB
                    t0 = i * 128
                    bank = [ps[4 * d + q] for q in range(4)]
                    lr_, gk_, gv_, gqT_, gkT_ = lr[d][b], gk[d][b], gv[d][b], gqT[d][b], gkT[d][b]
                    e1_, g1_, eblt_, kps_, eb_, enT_ = e1[d][0], g1[d][b], eblt[d][0], kps[d][b], eb[d][b], enT[d][0]
                    qpT_, kpT_, qpad_, atb_, osb_ = qpT[d][b], kpT[d][b], qpad[d][b], atb[d][b], osb[d][b]
                    pz = bank[0]
                    S.op("pe", lambda e: e.matmul(pz.t[:, :], lhsT=lr_.t[0:17, :], rhs=w2.t[:, d, :], start=True, stop=True),
                         reads=[lr_.res, w2.res], writes=[pz.res])
                    S.op("act", lambda e: e.activation(out=e1_.t[:], in_=pz.t[:, :], func=AF.Exp, scale=-1.0), reads=[pz.res], writes=[e1_.res])
                    S.op("act", lambda e: e.activation(out=g1_.t[:], in_=e1_.t[:], func=AF.Ln, bias=1.0), reads=[e1_.res], writes=[g1_.res])
                    yield
                    pbl = bank[1]
                    S.op("pe", lambda e: e.matmul(pbl.t[:, :], lhsT=tri2.t[:, 4 + d, :], rhs=g1_.t[:], start=True, stop=True),
                         reads=[tri2.res, g1_.res], writes=[pbl.res])
                    S.op("act", lambda e: e.activation(out=eblt_.t[:], in_=pbl.t[:, :], func=AF.Exp), reads=[pbl.res], writes=[eblt_.res])
                    S.op("dve", lambda e: e.tensor_tensor(out=kps_.t[:], in0=gk_.t[:], in1=eblt_.t[:], op=ALU.mult),
                         reads=[gk_.res, eblt_.res], writes=[kps_.res])
                    yield
                    pbT = bank[2]
                    for hh in range(4):
                        S.op("pe", lambda e: e.matmul(pbT.t[:, hh * 128:(hh + 1) * 128], lhsT=g1_.t[:, hh * 128:(hh + 1) * 128], rhs=tri2.t[:, 2 + d, :],
                                                      start=True, stop=True),
                             reads=[g1_.res, tri2.res], writes=[pbT.res], inc=(hh == 3))
                    S.op("act", lambda e: e.activation(out=eb_.t[:], in_=pbT.t[:, :], func=AF.Exp), reads=[pbT.res], writes=[eb_.res])
                    S.op("act", lambda e: e.activation(out=enT_.t[:], in_=pbT.t[:, :], func=AF.Exp, scale=-1.0), reads=[pbT.res], writes=[enT_.res])
                    yield
                    f2 = lambda ap: ap.rearrange("p a b -> p (a b)")
                    S.op("dve", lambda e: e.tensor_tensor(out=f2(qpT_.t[:]), in0=f2(gqT_.t[:]), in1=eb_.t[:], op=ALU.mult),
                         reads=[gqT_.res, eb_.res], writes=[qpT_.res])
                    S.op("pool", lambda e: e.tensor_tensor(out=f2(kpT_.t[:]), in0=f2(gkT_.t[:]), in1=enT_.t[:], op=ALU.mult),
                         reads=[gkT_.res, enT_.res], writes=[kpT_.res])
                    for c in range(2):
                        S.op("pool", lambda e: e.tensor_copy(out=qpad_.t[:, :, c, c * 64:(c + 1) * 64], in_=qpT_.t[:, :, c * 64:(c + 1) * 64]),
                             reads=[qpT_.res], writes=[qpad_.res])
                    yield
                    pat = bank[3]
                    for hh in range(4):
                        S.op("pe", lambda e: e.matmul(pat.t[:, hh * 128:(hh + 1) * 128], lhsT=kpT_.t[:, hh, :], rhs=qpT_.t[:, hh, :], start=True, stop=True),
                             reads=[kpT_.res, qpT_.res], writes=[pat.res], inc=(hh == 3))
                    S.op("dve", lambda e: e.tensor_tensor(out=atb_.t[:], in0=pat.t[:, :].rearrange("p (a b) -> p a b", a=4),
                                                          in1=tri2.t[:, d, :].unsqueeze(1).to_broadcast([128, 4, 128]), op=ALU.mult),
                         reads=[pat.res, tri2.res], writes=[atb_.res])
                    yield
                    corder = (0, 1) if d == 0 else (1, 0)
                    pkv = {}

                    def kv_mm(c):
                        for hp in range(2):
                            pk = bank[hp]
                            for h2 in range(2):
                                hh = hp * 2 + h2
                                pkv[(c, hh)] = (pk, h2)
                                S.op("pe", lambda e: e.matmul(pk.t[:, h2 * 256:(h2 + 1) * 256], lhsT=kps_.t[c * 64:(c + 1) * 64, hh * 128:(hh + 1) * 128],
                                                              rhs=gv_.t[c * 64:(c + 1) * 64, hh * 256:(hh + 1) * 256], start=True, stop=True),
                                     reads=[kps_.res, gv_.res], writes=[pk.res], inc=(h2 == 1))
                    kv_mm(corder[0])
                    yield
                    po = [bank[2], bank[3]]
                    c0_, c1_ = corder
                    for hh in range(4):
                        pb_, reg = po[hh // 2], (hh % 2) * 256
                        S.op("pe", lambda e: e.matmul(pb_.t[:, reg:reg + 256], lhsT=qpad_.t[:, hh, c0_, :], rhs=Sb2[d][hh][sver[d][hh]].t[:],
                                                      start=(hh % 2 == 0), stop=False, skip_group_check=True),
                             reads=[qpad_.res, Sb2[d][hh][sver[d][hh]].res], writes=[pb_.res], inc=False)
                        S.op("pe", lambda e: e.matmul(pb_.t[:, reg:reg + 256], lhsT=atb_.t[:, hh, :], rhs=gv_.t[:, hh * 256:(hh + 1) * 256],
                                                      start=False, stop=False, skip_group_check=True),
                             reads=[atb_.res, gv_.res], writes=[pb_.res], inc=(hh == 3))
                    yield
                    for ci, c in enumerate(corder):
                        col = (c * 64 + 63) if d == 0 else (c * 64)
                        for hh in range(4):
                            pk, h2 = pkv[(c, hh)]
                            ebl = eb_.t[:, hh * 128 + col:hh * 128 + col + 1]
                            S.op("dve", lambda e: e.scalar_tensor_tensor(out=Sf[d][hh].t[:], in0=Sf[d][hh].t[:], scalar=ebl, in1=pk.t[:, h2 * 256:(h2 + 1) * 256],
                                                                         op0=ALU.mult, op1=ALU.add),
                                 reads=[Sf[d][hh].res, eb_.res, pk.res], writes=[Sf[d][hh].res])
                            nv = 1 - sver[d][hh]
                            S.op("act", lambda e: e.activation(out=Sb2[d][hh][nv].t[:], in_=Sf[d][hh].t[:], func=AF.Copy), reads=[Sf[d][hh].res], writes=[Sb2[d][hh][nv].res])
                            sver[d][hh] = nv
                        yield
                        if ci == 0:
                            kv_mm(corder[1])
                            for hh in range(4):
                                pb_, reg = po[hh // 2], (hh % 2) * 256
                                S.op("pe", lambda e: e.matmul(pb_.t[:, reg:reg + 256], lhsT=qpad_.t[:, hh, c1_, :], rhs=Sb2[d][hh][sver[d][hh]].t[:],
                                                              start=False, stop=True, skip_group_check=True),
                                     reads=[qpad_.res, Sb2[d][hh][sver[d][hh]].res], writes=[pb_.res], inc=(hh == 3))
                    yield
                    S.op("act", lambda e: e.activation(out=osb_.t[:, 0:512], in_=po[0].t[:, :], func=AF.Copy), reads=[po[0].res], writes=[osb_.res])
                    S.op("dve", lambda e: e.tensor_copy(out=osb_.t[:, 512:1024], in_=po[1].t[:, :]), reads=[po[1].res], writes=[osb_.res])
                    S.dma("sp", O_d[d][t0:t0 + 128, :], osb_.t[:], osb_.res, reads=[osb_.res], dwrites=[dres["OF" if d == 0 else "OB"]])

                gla_load(0, fwd[0], 0)
                gla_load(1, bwd[0], 0)
                for n in range(NTILE):
                    if n + 1 < NTILE:
                        gla_load(0, fwd[n + 1], n + 1)
                        gla_load(1, bwd[n + 1], n + 1)
                    gens = [gla_tile(0, fwd[n], n), gla_tile(1, bwd[n], n)]
                    while gens:
                        for g_ in list(gens):
                            try:
                                next(g_)
                            except StopIteration:
                                gens.remove(g_)
                S.barrier()

            wst = ExitStack()
            wbr = [S.tile(wst, "wbr%d" % i, [128, 8, D], BF16) for i in range(3)]
            wbr_q = [[S.res("wbrq%d_%d" % (br, q)) for q in range(4)] for br in range(3)]
            for q in range(4):
                for br in range(3):
                    S.dma("pool", wbr[br].t[:, :, q * 512:(q + 1) * 512],
                          wb_d[br][l, :, q * 512:(q + 1) * 512].rearrange("(k p) n -> p k n", p=128),
                          wbr_q[br][q], writes=[wbr_q[br][q]])
            with ExitStack() as st2:
                PD = 3
                oa = [S.tile(st2, "oa%d" % i, [128, 1024], F32) for i in range(PD)]
                obb = [S.tile(st2, "obb%d" % i, [128, 1024], F32) for i in range(PD)]
                sgg = [S.tile(st2, "sgg%d" % i, [128, 1024], BF16) for i in range(PD)]
                sqs_ = [S.tile(st2, "gsq%d" % i, [128, 1024], F32) for i in range(PD)]
                gob = [S.tile(st2, "gob%d" % i, [128, 1024], BF16) for i in range(PD)]
                s8 = [S.tile(st2, "gs8%d" % i, [128, 16], F32) for i in range(PD)]
                goT = [S.tile(st2, "goT%d" % i, [128, 8, 128], BF16) for i in range(PD)]
                v4 = lambda ap: ap.rearrange("p (g d) -> p g d", g=4)
                gn = rowp.t[:, RP + 256:RP + 512]
                def fp_tile(i):
                    b = i % PD
                    oa_, ob_, sg_, go_, s8_, gT_, sq_ = oa[b], obb[b], sgg[b], gob[b], s8[b], goT[b], sqs_[b]
                    S.dma("sp", oa_.t[:], O_d[0][i * 128:(i + 1) * 128, :], oa_.res, reads=[dres["OF"]], writes=[oa_.res])
                    S.dma("sp", ob_.t[:], O_d[1][i * 128:(i + 1) * 128, :], ob_.res, reads=[dres["OB"]], writes=[ob_.res])
                    S.dma("sp", sg_.t[:], SGG_d[i * 128:(i + 1) * 128, :], sg_.res, reads=[dres["SGG"]], writes=[sg_.res])
                    S.op("pool", lambda e: e.tensor_tensor(out=oa_.t[:], in0=oa_.t[:], in1=ob_.t[:], op=ALU.add),
                         reads=[oa_.res, ob_.res], writes=[oa_.res])
                    yield
                    S.op("act", lambda e: e.activation(out=sq_.t[:], in_=oa_.t[:], func=AF.Square), reads=[oa_.res], writes=[sq_.res])
                    yield
                    S.op("dve", lambda e: e.reduce_sum(out=s8_.t[:, 0:4], in_=v4(sq_.t[:]), axis=AX.X), reads=[sq_.res], writes=[s8_.res])
                    S.op("dve", lambda e: e.tensor_scalar(out=s8_.t[:, 0:4], in0=s8_.t[:, 0:4], scalar1=1.0 / 256, scalar2=EPS,
                                                          op0=ALU.mult, op1=ALU.add), reads=[s8_.res], writes=[s8_.res])
                    yield
                    S.op("act", lambda e: e.activation(out=s8_.t[:, 4:8], in_=s8_.t[:, 0:4], func=AF.Sqrt), reads=[s8_.res], writes=[s8_.res])
                    yield
                    S.op("dve", lambda e: e.reciprocal(out=s8_.t[:, 8:12], in_=s8_.t[:, 4:8]), reads=[s8_.res], writes=[s8_.res])
                    S.op("dve", lambda e: e.tensor_tensor(out=v4(oa_.t[:]), in0=v4(oa_.t[:]),
                                                          in1=s8_.t[:, 8:12].unsqueeze(2).to_broadcast([128, 4, 256]), op=ALU.mult),
                         reads=[oa_.res, s8_.res], writes=[oa_.res])
                    yield
                    S.op("pool", lambda e: e.tensor_tensor(out=v4(oa_.t[:]), in0=v4(oa_.t[:]),
                                                           in1=gn.unsqueeze(1).to_broadcast([128, 4, 256]), op=ALU.mult),
                         reads=[oa_.res, rowp.res], writes=[oa_.res])
                    yield
                    S.op("dve", lambda e: e.tensor_tensor(out=go_.t[:], in0=oa_.t[:], in1=sg_.t[:], op=ALU.mult),
                         reads=[oa_.res, sg_.res], writes=[go_.res])
                    yield
                    p2 = next_ps()
                    pb = p2.t[:].bitcast(BF16)
                    for cb in range(8):
                        S.op("pe", lambda e: e.transpose(out=pb[:, cb * 128:(cb + 1) * 128], in_=go_.t[:, cb * 128:(cb + 1) * 128],
                                                         identity=identb.t[:]),
                             reads=[go_.res, identb.res], writes=[p2.res], inc=(cb == 7))
                    yield
                    S.op("act", lambda e: e.activation(out=gT_.t[:].rearrange("p a b -> p (a b)"), in_=pb[:, 0:1024], func=AF.Copy),
                         reads=[p2.res], writes=[gT_.res])
                    S.dma("sp", GOT_d[:, :, i * 128:(i + 1) * 128].rearrange("c p t -> p c t"), gT_.t[:], gT_.res,
                          reads=[gT_.res], dwrites=[dres["GOT"]])

                active = []
                tiles_fp = list(range(0 if need_ctx else 2, NTILE))
                nxt = 0
                rnd = 0
                while nxt < len(tiles_fp) or active:
                    if nxt < len(tiles_fp) and len(active) < PD and rnd % 3 == 0:
                        active.append(fp_tile(tiles_fp[nxt]))
                        nxt += 1
                    rnd += 1
                    for g_ in list(active):
                        try:
                            next(g_)
                        except StopIteration:
                            active.remove(g_)
                S.barrier()
            if stop_after == "F":
                wst.close()
                return nc

            groups = TOK_GROUPS if need_ctx else [(256, 512), (768, 512), (1280, 512), (1792, 512)]
            with ExitStack() as st:
                xT = [[S.tile(st, "xT%d_%d" % (br, i), [128, 8, 512], BF16) for i in range(2)] for br in range(3)]
                mgt = [S.tile(st, "mgt%d" % i, [128, 3, 512], BF16) for i in range(3)]
                tm = [[S.tile(st, "tm%d_%d" % (br, i), [128, 512], F32) for i in range(2)] for br in range(3)]
                ys = [S.tile(st, "ys%d" % i, [128, 512], BF16) for i in range(3)]
                MG3 = MGT_d.rearrange("(b f) p t -> f p b t", b=3)
                srcs = (POT_d, DOT_d, GOT_d)
                names = ("POT", "DOT", "GOT")

                def load_x(gi):
                    t0, tsz = groups[gi]
                    for br in range(3):
                        S.dma("sp", xT[br][gi % 2].t[:, :, 0:tsz], srcs[br][:, :, t0:t0 + tsz].rearrange("c p t -> p c t"), xT[br][gi % 2].res,
                              reads=[dres[names[br]]], writes=[xT[br][gi % 2].res])
                flat = [(gi, fb) for gi in range(len(groups)) for fb in range(16)]

                def load_mg(n):
                    gi, fb = flat[n]
                    t0, tsz = groups[gi]
                    S.dma("sp", mgt[n % 3].t[:, :, 0:tsz], MG3[fb, :, :, t0:t0 + tsz], mgt[n % 3].res, reads=[dres["MGT"]], writes=[mgt[n % 3].res])
                load_x(0)
                load_mg(0)
                load_mg(1)
                for n, (gi, fb) in enumerate(flat):
                    t0, tsz = groups[gi]
                    gb_i = gi % 2
                    if fb == 0 and gi + 1 < len(groups):
                        load_x(gi + 1)
                    if n + 2 < len(flat):
                        load_mg(n + 2)
                    mg_ = mgt[n % 3]
                    ys_ = ys[n % 3]
                    pts = []
                    for br in range(3):
                        pt = next_ps()
                        pts.append(pt)
                        for k in range(8):
                            S.op("pe", lambda e: e.matmul(pt.t[:, 0:tsz], lhsT=wbr[br].t[:, k, fb * 128:(fb + 1) * 128],
                                                          rhs=xT[br][gb_i].t[:, k, 0:tsz], start=(k == 0), stop=(k == 7)),
                                 reads=[wbr_q[br][fb // 4], xT[br][gb_i].res], writes=[pt.res], inc=(k == 7))
                    tms = [tm[br][n % 2] for br in range(3)]
                    for br in range(3):
                        S.op("dve", lambda e: e.tensor_tensor(out=tms[br].t[:, 0:tsz], in0=pts[br].t[:, 0:tsz], in1=mg_.t[:, br, 0:tsz], op=ALU.mult),
                             reads=[pts[br].res, mg_.res], writes=[tms[br].res])
                    S.op("pool", lambda e: e.tensor_tensor(out=tms[0].t[:, 0:tsz], in0=tms[0].t[:, 0:tsz], in1=tms[1].t[:, 0:tsz], op=ALU.add),
                         reads=[tms[0].res, tms[1].res], writes=[tms[0].res])
                    S.op("pool", lambda e: e.tensor_tensor(out=ys_.t[:, 0:tsz], in0=tms[0].t[:, 0:tsz], in1=tms[2].t[:, 0:tsz], op=ALU.add),
                         reads=[tms[0].res, tms[2].res], writes=[ys_.res])
                    S.dma("sp", YT_d[fb, :, t0:t0 + tsz], ys_.t[:, 0:tsz], ys_.res, reads=[ys_.res], dwrites=[dres["YT"]])
                S.barrier()
            wst.close()
            if stop_after == "G1":
                return nc

            with ExitStack() as st:
                wo = S.tile(st, "wo", [128, 16, D], BF16)
                gb = [S.tile(st, "gb%d" % r, [128, D], F32) for r in range(2)]
                dg_ = S.tile(st, "dgm", [128, 128], F32)
                wo_q = [S.res("woq%d" % q) for q in range(4)]
                for hf in range(4):
                    S.dma("pool", wo.t[:, :, hf * 512:(hf + 1) * 512], w_out[l, :, hf * 512:(hf + 1) * 512].rearrange("(k p) n -> p k n", p=128),
                          wo_q[hf], writes=[wo_q[hf]])
                for r in range(2):
                    for k in range(16):
                        S.op("dve", lambda e: e.tensor_scalar(out=dg_.t[:], in0=ident.t[:], scalar1=mod.t[:, l * 48 + 32 + k, r:r + 1], scalar2=None,
                                                              op0=ALU.mult), reads=[ident.res, mod.res], writes=[dg_.res])
                        pt = next_ps()
                        S.op("pe", lambda e: e.matmul(pt.t[:, 0:128], lhsT=ones_f.t[:], rhs=dg_.t[:], start=True, stop=True),
                             reads=[ones_f.res, dg_.res], writes=[pt.res])
                        S.op("act", lambda e: e.activation(out=gb[r].t[:, k * 128:(k + 1) * 128], in_=pt.t[:, 0:128], func=AF.Copy),
                             reads=[pt.res], dwrites=[gb[r].res])
                yT = [S.tile(st, "yT%d" % i, [128, 16, 512], BF16) for i in range(2)]
                xr = [S.tile(st, "xr%d" % i, [128, D], F32) for i in range(2)]
                xo = [S.tile(st, "xo%d" % i, [128, D], F32) for i in range(2)]
                tg_ = [S.tile(st, "tgm%d" % i, [128, 512], F32) for i in range(2)]
                tc = 0
                tiles_g2 = [(gi, ti) for gi, (t0, tsz) in enumerate(groups) for ti in range(tsz // 128)]

                def load_y(gi):
                    t0, tsz = groups[gi]
                    S.dma("sp", yT[gi % 2].t[:, :, 0:tsz], YT_d[:, :, t0:t0 + tsz].rearrange("c p t -> p c t"), yT[gi % 2].res, reads=[dres["YT"]], writes=[yT[gi % 2].res])

                def load_xr(m):
                    gi, ti = tiles_g2[m]
                    i = groups[gi][0] // 128 + ti
                    S.dma("sp", xr[i % 2].t[:], xsrc[i * 128:(i + 1) * 128, :], xr[i % 2].res, reads=[dres["X1"]] if l > 0 else [], writes=[xr[i % 2].res])
                load_y(0)
                load_xr(0)
                for m, (gi, ti) in enumerate(tiles_g2):
                    t0, tsz = groups[gi]
                    y_ = yT[gi % 2]
                    if ti == 0 and gi + 1 < len(groups):
                        load_y(gi + 1)
                    if m + 1 < len(tiles_g2):
                        load_xr(m + 1)
                    if True:
                        i = t0 // 128 + ti
                        r = 1 if i < 2 else 0
                        xr_, xo_ = xr[i % 2], xo[i % 2]
                        for nb in range(4):
                            pt = next_ps()
                            for k in range(16):
                                S.op("pe", lambda e: e.matmul(pt.t[:, :], lhsT=y_.t[:, k, ti * 128:(ti + 1) * 128], rhs=wo.t[:, k, nb * 512:(nb + 1) * 512],
                                                              start=(k == 0), stop=(k == 15)),
                                     reads=[y_.res, wo_q[nb]], writes=[pt.res], inc=(k == 15))
                            tc += 1
                            tgx = tg_[tc % 2]
                            S.op("dve", lambda e: e.tensor_tensor(out=tgx.t[:], in0=pt.t[:], in1=gb[r].t[:, nb * 512:(nb + 1) * 512], op=ALU.mult),
                                 reads=[pt.res, gb[r].res], writes=[tgx.res])
                            S.op("pool", lambda e: e.tensor_tensor(out=xo_.t[:, nb * 512:(nb + 1) * 512], in0=tgx.t[:], in1=xr_.t[:, nb * 512:(nb + 1) * 512],
                                                                   op=ALU.add),
                                 reads=[tgx.res, xr_.res], writes=[xo_.res])
                        if l == n_layers - 1:
                            S.dma("sp", y_out[(i - 2) * 128:(i - 1) * 128, :], xo_.t[:], xo_.res, reads=[xo_.res], dwrites=[dres["Y"]])
                        else:
                            S.dma("sp", X1_d[i * 128:(i + 1) * 128, :], xo_.t[:], xo_.res, reads=[xo_.res], dwrites=[dres["X1"]])
                S.barrier()
        S.barrier()
        print("program built: inst=%d waits=%d sems=%d" % (S.ninst, S.nwaits, S.nsem))
    return nc


_CONSTS = None


def _consts():
    global _CONSTS
    if _CONSTS is None:
        _CONSTS = {
            "ident": np.eye(128, dtype=np.float32),
            "band": _band_tables(),
            "rope": _rope_tables(),
            "tri": _tri_tables(),
            "tri2": _tri2_tables(),
        }
    return _CONSTS


def _host_inputs(inputs, b):
    f = lambda a: np.ascontiguousarray(np.asarray(a, dtype=np.float32))
    x, ctx, c, c_ctx = f(inputs["x"]), f(inputs["ctx"]), f(inputs["c"]), f(inputs["c_ctx"])
    xin = np.concatenate([ctx[b], x[b]], axis=0)
    cT = np.stack([c[b].reshape(16, 128).T, c_ctx.reshape(16, 128).T], axis=-1).reshape(128, 32)
    colp = np.zeros((128, 2, 72), np.float32)
    rowp = np.zeros((2, 768), np.float32)
    w2 = np.zeros((2, 2, 17, 512), np.float32)
    for l in range(2):
        colp[:, l, 0:16] = f(inputs["norm_g"])[l].reshape(16, 128).T
        colp[:, l, 16:64] = f(inputs["b_ada"])[l].reshape(48, 128).T
        colp[:, l, 64:72] = f(inputs["pool_scale"])[l].reshape(8, 128).T
        rowp[l, 0:64] = f(inputs["diff_q_norm"])[l]
        rowp[l, 64:128] = f(inputs["diff_k_norm"])[l]
        rowp[l, 128:256] = f(inputs["diff_subln"])[l]
        rowp[l, 256:512] = f(inputs["gla_norm"])[l]
        rowp[l, 512:576] = f(inputs["diff_lam_q1"])[l]
        rowp[l, 576:640] = f(inputs["diff_lam_k1"])[l]
        rowp[l, 640:704] = f(inputs["diff_lam_q2"])[l]
        rowp[l, 704:768] = f(inputs["diff_lam_k2"])[l]
        w2[l, 0, 0:16] = f(inputs["gla_w_gate_f"])[l]
        w2[l, 0, 16] = f(inputs["gla_b_gate_f"])[l]
        w2[l, 1, 0:16] = f(inputs["gla_w_gate_b"])[l]
        w2[l, 1, 16] = f(inputs["gla_b_gate_b"])[l]
    m = {
        "xin": np.ascontiguousarray(xin), "cT": np.ascontiguousarray(cT),
        "w_ada": f(inputs["w_ada"]), "w_in": f(inputs["w_in"]), "pool_w": f(inputs["pool_w"]),
        "w_branch_pool": f(inputs["w_branch_pool"]), "w_branch_diff": f(inputs["w_branch_diff"]),
        "w_branch_gla": f(inputs["w_branch_gla"]), "w_out": f(inputs["w_out"]),
        "colp": np.ascontiguousarray(colp.reshape(128, 144)), "rowp": rowp, "w2": w2,
    }
    m.update(_consts())
    return m


def kernel(**inputs):
    nc = build_program()
    in_maps = [_host_inputs(inputs, b) for b in range(N_ACTIVE)]
    res = run_bass_kernel_spmd(nc, in_maps, core_ids=list(range(N_ACTIVE)))
    out = np.stack([np.asarray(res.results[b]["y"], dtype=np.float32) for b in range(N_ACTIVE)], axis=0)
    return out
```

```python
import math
from contextlib import ExitStack

import numpy as np
import concourse.bass as bass
import concourse.mybir as mybir
from concourse.bass_utils import run_bass_kernel_spmd

F32 = mybir.dt.float32
BF16 = mybir.dt.bfloat16
AF = mybir.ActivationFunctionType
ALU = mybir.AluOpType
AX = mybir.AxisListType

D = 2048
LX = 2048
LC = 256
NT = LX + LC
NTILE = NT // 128
EPS = 1e-6
D_IN = 15392
N_ACTIVE = 4


class Res:
    __slots__ = ("name", "w", "r", "dsem", "excl")

    def __init__(self, name, sched=None):
        self.name = name
        self.excl = False
        self.w = {}
        self.r = {}
        self.dsem = None
        if sched is not None:
            sched.all_res.append(self)


class Tile:
    __slots__ = ("t", "res")

    def __init__(self, t, res):
        self.t = t
        self.res = res


class Sched:
    def __init__(self, nc, stack):
        self.nc = nc
        self.stack = stack
        self.eng = {"pe": nc.tensor, "act": nc.scalar, "dve": nc.vector, "pool": nc.gpsimd, "sp": nc.sync}
        self.sem = {k: stack.enter_context(nc.semaphore("s_" + k)) for k in self.eng}
        self.cnt = {k: 0 for k in self.eng}
        self.seen = {k: {} for k in self.eng}
        self.pend = {k: ([], [], []) for k in self.eng}
        self.dtotal = {}
        self.free_dsems = {}
        self.dkind = {}
        self.all_res = []
        self.nsem = 5
        self.nwaits = 0
        self.ninst = 0
        self._junk = self.tile(stack, "junk_b", [128, 8], F32)

    def tile(self, stack, name, shape, dtype):
        self.ntile = getattr(self, "ntile", 0) + 1
        name = "t%d_%s" % (self.ntile, name)
        t = stack.enter_context(self.nc.sbuf_tensor(name, list(shape), dtype))
        return Tile(t, Res(name, self))

    def res(self, name):
        return Res(name, self)

    def _dsem(self, res, q="sp"):
        if res.dsem is None:
            kind = "sw" if q == "pool" else "hw"
            fl = self.free_dsems.setdefault(kind, [])
            if fl:
                res.dsem = fl.pop()
            else:
                res.dsem = self.stack.enter_context(self.nc.semaphore("d%d" % self.nsem))
                self.nsem += 1
                self.dtotal[res.dsem] = 0
                self.dkind[res.dsem] = kind
        return res.dsem

    def _wait_for(self, e, reads, writes, dwrites):
        need = {}
        for r in reads:
            for s, v in r.w.items():
                if need.get(s, 0) < v:
                    need[s] = v
        for w in writes:
            for s, v in w.w.items():
                if need.get(s, 0) < v:
                    need[s] = v
            for s, v in w.r.items():
                if need.get(s, 0) < v:
                    need[s] = v
        for w in dwrites:
            for s, v in w.r.items():
                if need.get(s, 0) < v:
                    need[s] = v
        own = self.sem[e]
        seen = self.seen[e]
        for s, v in need.items():
            if e == "pe" and s is own:
                continue
            if s in self.dtotal:
                v = max(v, self.dtotal[s])
            if seen.get(s, 0) >= v:
                continue
            self.eng[e].wait_ge(s, v)
            self.nwaits += 1
            seen[s] = v

    def _record(self, ev, reads, writes, dwrites):
        s, v = ev
        for r in reads:
            if r.r.get(s, 0) < v:
                r.r[s] = v
        for w in writes:
            w.w = {s: v}
            w.r = {}
        for w in dwrites:
            w.w[s] = v

    def op(self, e, fn, reads=(), writes=(), dwrites=(), inc=True):
        if any(r.excl for r in reads):
            writes = list(writes) + [r for r in reads if r.excl]
            reads = [r for r in reads if not r.excl]
        self._wait_for(e, reads, writes, dwrites)
        inst = fn(self.eng[e])
        self.ninst += 1
        if not inc:
            p = self.pend[e]
            p[0].extend(reads)
            p[1].extend(writes)
            p[2].extend(dwrites)
            return
        self.cnt[e] += 1
        inst.then_inc(self.sem[e], 1)
        ev = (self.sem[e], self.cnt[e])
        p = self.pend[e]
        self._record(ev, list(reads) + p[0], list(writes) + p[1], list(dwrites) + p[2])
        self.pend[e] = ([], [], [])

    def dma(self, q, out_ap, in_ap, sem_res, reads=(), writes=(), dwrites=()):
        self._wait_for(q, reads, writes, dwrites)
        inst = self.eng[q].dma_start(out=out_ap, in_=in_ap)
        self.ninst += 1
        s = self._dsem(sem_res, q)
        self.dtotal[s] += 16
        inst.then_inc(s, 16)
        self._record((s, self.dtotal[s]), reads, writes, dwrites)

    def barrier(self):
        for e in self.eng:
            assert not any(self.pend[e]), "pending ops at barrier on " + e
        e = "dve"
        seen = self.seen[e]
        for e2 in self.eng:
            s, v = self.sem[e2], self.cnt[e2]
            if v > seen.get(s, 0):
                self.eng[e].wait_ge(s, v)
                seen[s] = v
        for s, v in self.dtotal.items():
            if v > seen.get(s, 0):
                self.eng[e].wait_ge(s, v)
                seen[s] = v
        inst = self.eng[e].memset(self._junk.t[:, 0:1], 0.0)
        self.cnt[e] += 1
        inst.then_inc(self.sem[e], 1)
        v = self.cnt[e]
        seen[self.sem[e]] = v
        for e2 in self.eng:
            if e2 == e:
                continue
            self.eng[e2].wait_ge(self.sem[e], v)
            for s, vv in seen.items():
                if self.seen[e2].get(s, 0) < vv:
                    self.seen[e2][s] = vv
        for r in self.all_res:
            r.w = {}
            r.r = {}
            if r.dsem is not None:
                self.free_dsems.setdefault(self.dkind[r.dsem], []).append(r.dsem)
                r.dsem = None


POOL_WINDOWS = (2, 4, 8, 16)


def _band_tables():
    out = np.zeros((128, 4, 5, 128), np.float32)
    s = np.arange(128)[:, None]
    t = np.arange(128)[None, :]
    for gi, w in enumerate(POOL_WINDOWS):
        h = w // 2
        inwin = lambda ss: ((ss >= t - h) & (ss < t + h)).astype(np.float32)
        eye = (s == t).astype(np.float32)
        out[:, gi, 0] = inwin(s - 128) / w
        out[:, gi, 1] = inwin(s) / w - eye
        out[:, gi, 2] = inwin(s + 128) / w
        cnt_first = np.minimum(t + h, 10 ** 9) - np.maximum(t - h, 0)
        out[:, gi, 3] = inwin(s) / cnt_first - eye
        cnt_last = np.minimum(t + h, 128) - (t - h)
        out[:, gi, 4] = inwin(s) / cnt_last - eye
    return out.reshape(128, 20 * 128)


def _rope_tables():
    n_freq = 16
    inv = (10000.0 ** (-np.arange(n_freq, dtype=np.float32) / n_freq)).astype(np.float32)
    tt = np.arange(LX)
    rows = (tt // 64).astype(np.float32)
    cols = (tt % 64).astype(np.float32)
    ang = np.concatenate([rows[:, None] * inv, cols[:, None] * inv], axis=-1).astype(np.float32)
    cos = np.cos(ang).astype(np.float32).reshape(16, 128, 32).transpose(1, 0, 2)
    sin = np.sin(ang).astype(np.float32).reshape(16, 128, 32).transpose(1, 0, 2)
    return np.ascontiguousarray(np.concatenate([cos, sin], axis=1).reshape(128, 32 * 32))


def _tri_tables():
    s = np.arange(64)[:, None]
    t = np.arange(64)[None, :]
    mf = (s <= t).astype(np.float32)
    mb = (s >= t).astype(np.float32)
    out = np.zeros((64, 4, 64), np.float32)
    out[:, 0] = mf
    out[:, 1] = mb
    out[:, 2] = -mf / 16.0
    out[:, 3] = -mb / 16.0
    return out.reshape(64, 256)


def _tri2_tables():
    s = np.arange(128)[:, None]
    t = np.arange(128)[None, :]
    same = ((s // 64) == (t // 64)).astype(np.float32)
    out = np.zeros((128, 6, 128), np.float32)
    out[:, 0] = same * (s <= t)
    out[:, 1] = same * (s >= t)
    out[:, 2] = -(same * (s <= t)) / 16.0
    out[:, 3] = -(same * (s >= t)) / 16.0
    out[:, 4] = -(same * (s > t)) / 16.0
    out[:, 5] = -(same * (s < t)) / 16.0
    return out.reshape(128, 6 * 128)


_G = {}
_off = 0
for _n, _s in (("pu", 1024), ("pg", 1024), ("dq", 1024), ("dk", 1024), ("dv", 1024), ("dg", 1024),
               ("gq", 512), ("gk", 512), ("gv", 1024), ("gg", 1024), ("glf", 16), ("glb", 16), ("mg", 6144)):
    _G[_n] = (_off, _s)
    _off += _s
assert _off == D_IN

TOK_GROUPS = [(0, 512), (512, 512), (1024, 512), (1536, 512), (2048, 256)]


def build_program(n_layers=2, dbg=False, stop_after=None):
    nc = bass.Bass("TRN2", target_bir_lowering=False)

    def din(name, shape, dt=F32):
        return nc.dram_tensor(name, list(shape), dt, kind="ExternalInput").ap()

    scratch_kind = "ExternalOutput" if dbg else "Internal"

    def dscr(name, shape, dt):
        return nc.dram_tensor(name, list(shape), dt, kind=scratch_kind).ap()

    xin = din("xin", [NT, D])
    cT_d = din("cT", [128, 32])
    w_ada = din("w_ada", [2, D, 3 * D])
    w_in = din("w_in", [2, D, D_IN])
    pool_w = din("pool_w", [2, 4, 256, 256])
    wb_d = [din("w_branch_pool", [2, 1024, D]), din("w_branch_diff", [2, 1024, D]), din("w_branch_gla", [2, 1024, D])]
    w_out = din("w_out", [2, D, D])
    colp_d = din("colp", [128, 2 * 72])
    rowp_d = din("rowp", [2, 768])
    w2_d = din("w2", [2, 2, 17, 512])
    ident_d = din("ident", [128, 128])
    band_d = din("band", [128, 20 * 128])
    rope_d = din("rope", [128, 32 * 32])
    tri_d = din("tri", [64, 256])
    tri2_d = din("tri2", [128, 6 * 128])
    y_out = nc.dram_tensor("y", [LX, D], F32, kind="ExternalOutput").ap()

    U_d = dscr("U_d", [NT, 1024], BF16)
    SPG_d = dscr("SPG_d", [8, 128, NT], BF16)
    QT_d = dscr("QT_d", [8, 128, NT], BF16)
    KT_d = dscr("KT_d", [8, 128, NT], BF16)
    V_d = dscr("V_d", [NT, 1024], BF16)
    SDG_d = dscr("SDG_d", [NT, 1024], BF16)
    GQT_d = dscr("GQT_d", [4, 128, NT], F32)
    GKT_d = dscr("GKT_d", [4, 128, NT], F32)
    GK_d = dscr("GK_d", [NT, 512], F32)
    GV_d = dscr("GV_d", [NT, 1024], BF16)
    SGG_d = dscr("SGG_d", [NT, 1024], BF16)
    LRT_d = dscr("LRT_d", [2, 16, NT], F32)
    MGT_d = dscr("MGT_d", [48, 128, NT], BF16)
    POT_d = dscr("POT_d", [8, 128, NT], BF16)
    DOT_d = dscr("DOT_d", [8, 128, NT], BF16)
    GOT_d = dscr("GOT_d", [8, 128, NT], BF16)
    O_d = [dscr("OF_d", [NT, 1024], F32), dscr("OB_d", [NT, 1024], F32)]
    X1_d = dscr("X1_d", [NT, D], F32)
    YT_d = dscr("YT_d", [16, 128, NT], BF16)
    DOtm_d = dscr("DOtm_d", [NT, 1024], BF16)
    HT_dbg = dscr("HT_dbg", [16, 128, NT], BF16) if dbg else None
    MOD_dbg = dscr("MOD_dbg", [128, 192], F32) if dbg else None

    with ExitStack() as gs_:
        S = Sched(nc, gs_)
        psall = gs_.enter_context(nc.psum_tensor("psall", [128, 4096], F32))
        ps = [Tile(psall[:, i * 512:(i + 1) * 512], S.res("ps%d" % i)) for i in range(8)]
        for p_ in ps:
            p_.res.excl = True
        ps_rr = [0]

        def next_ps(lo=0, hi=8):
            i = ps_rr[0]
            ps_rr[0] = i + 1
            return ps[lo + i % (hi - lo)]

        dres = {n: S.res("dr_" + n) for n in ("U", "SPG", "QT", "KT", "V", "SDG", "GQT", "GKT", "GK", "GV", "SGG", "LRT",
                                               "MGT", "POT", "DOT", "GOT", "OF", "OB", "X1", "Y", "DBG", "YT", "DOTM")}

        G = gs_
        ident = S.tile(G, "ident", [128, 128], F32)
        identb = S.tile(G, "identb", [128, 128], BF16)
        ones_f = S.tile(G, "ones_f", [128, 128], F32)
        colp = S.tile(G, "colp", [128, 144], F32)
        rowp = S.tile(G, "rowp", [128, 2 * 768], F32)
        band = S.tile(G, "band", [128, 20, 128], BF16)
        rope = S.tile(G, "rope", [128, 32, 32], F32)
        tri = S.tile(G, "tri", [64, 4, 64], F32)
        trib = S.tile(G, "trib", [64, 2, 64], BF16)
        mod = S.tile(G, "mod", [128, 96, 2], F32)
        gsc = S.tile(G, "gsc", [128, 32, 2], F32)
        small = S.tile(G, "small", [128, 64], F32)

        S.dma("sp", ident.t[:], ident_d, ident.res, writes=[ident.res])
        S.dma("pool", identb.t[:], ident_d, identb.res, writes=[identb.res])
        S.dma("sp", colp.t[:], colp_d, colp.res, writes=[colp.res])
        S.dma("sp", rowp.t[:].rearrange("p (l n) -> p l n", l=2), rowp_d.partition_broadcast(128), rowp.res, writes=[rowp.res])
        S.dma("pool", band.t[:].rearrange("p a b -> p (a b)"), band_d, band.res, writes=[band.res])
        S.dma("sp", rope.t[:].rearrange("p a b -> p (a b)"), rope_d, rope.res, writes=[rope.res])
        S.dma("sp", tri.t[:].rearrange("p a b -> p (a b)"), tri_d, tri.res, writes=[tri.res])
        S.dma("pool", trib.t[:].rearrange("p a b -> p (a b)"), tri_d[:, 0:128], trib.res, writes=[trib.res])
        S.op("dve", lambda e: e.memset(ones_f.t[:], 1.0), writes=[ones_f.res])

        sct = S.tile(G, "sct", [128, 32], BF16)

        def ada_dma(l_, j, wb):
            S.dma("pool", wb.t[:], w_ada[l_, :, j * 512:(j + 1) * 512].rearrange("(k p) n -> p k n", p=128),
                  wb.res, writes=[wb.res])

        def ada_block(l_, j, wb, banks):
            for m in range(4):
                pt = banks[m % len(banks)] if banks else next_ps()
                for k in range(16):
                    S.op("pe", lambda e: e.matmul(pt.t[:, 0:2], lhsT=wb.t[:, k, m * 128:(m + 1) * 128],
                                                  rhs=sct.t[:, 2 * k:2 * k + 2], start=(k == 0), stop=(k == 15)),
                         reads=[wb.res, sct.res], writes=[pt.res], inc=(k == 15))
                col = j * 4 + m
                S.op("act", lambda e: e.activation(out=mod.t[:, l_ * 48 + col, :], in_=pt.t[:, 0:2], func=AF.Identity,
                                                   bias=colp.t[:, l_ * 72 + 16 + col:l_ * 72 + 17 + col], scale=1.0),
                     reads=[pt.res, colp.res], dwrites=[mod.res])

        def ada_finish(l_):
            for r in range(2):
                S.op("dve", lambda e: e.scalar_tensor_tensor(out=gsc.t[:, l_ * 16:(l_ + 1) * 16, r],
                                                             in0=mod.t[:, l_ * 48 + 16:l_ * 48 + 32, r], scalar=1.0,
                                                             in1=colp.t[:, l_ * 72:l_ * 72 + 16], op0=ALU.add, op1=ALU.mult),
                     reads=[mod.res, colp.res], dwrites=[gsc.res])

        with ExitStack() as st:
            ct = S.tile(st, "ct", [128, 32], F32)
            wab = [S.tile(st, "wab%d" % i, [128, 16, 512], BF16) for i in range(2)]
            S.dma("sp", ct.t[:], cT_d, ct.res, writes=[ct.res])
            S.op("act", lambda e: e.activation(out=sct.t[:], in_=ct.t[:], func=AF.Silu), reads=[ct.res], writes=[sct.res])
            ada_dma(0, 0, wab[0])
            for j in range(12):
                if j + 1 < 12:
                    ada_dma(0, j + 1, wab[(j + 1) % 2])
                ada_block(0, j, wab[j % 2], None)
            ada_finish(0)
            if dbg:
                S.dma("sp", MOD_dbg, mod.t[:].rearrange("p a b -> p (a b)"), mod.res, reads=[mod.res], writes=[dres["DBG"]])
            S.barrier()
        if stop_after == "A":
            S.barrier()
            return nc

        for l in range(n_layers):
            need_ctx = l < n_layers - 1
            xsrc = xin if l == 0 else X1_d
            lam_init = 0.8 - 0.6 * math.exp(-0.3 * l)
            RP = l * 768

            with ExitStack() as lst:
                hT = S.tile(lst, "hT", [128, 16, NT], BF16)
                hres = [S.res("hT%d" % i) for i in range(NTILE)]

                wbuf = [S.tile(lst, "wbuf%d" % i, [128, 16, 512], BF16) for i in range(3)]
                preload = {}
                for pi_, c0_ in enumerate((0, 512)):
                    S.dma("pool", wbuf[pi_].t[:, :, 0:512], w_in[l, :, c0_:c0_ + 512].rearrange("(k p) n -> p k n", p=128),
                          wbuf[pi_].res, writes=[wbuf[pi_].res])
                    preload[(c0_, 512)] = wbuf[pi_]

                with ExitStack() as st:
                    BD = 3
                    xb = [S.tile(st, "xb%d" % i, [128, D], F32) for i in range(BD)]
                    junk = S.tile(st, "junk", [128, D], BF16)
                    ssb = [S.tile(st, "ssb%d" % i, [128, 4], F32) for i in range(BD)]

                    def b_tile(i):
                        r = 1 if i < 2 else 0
                        xt = xb[i % BD]
                        ss = ssb[i % BD]
                        S.dma("sp", xt.t[:], xsrc[i * 128:(i + 1) * 128, :], xt.res, reads=[dres["X1"]], writes=[xt.res])
                        S.op("act", lambda e: e.activation(out=junk.t[:], in_=xt.t[:], func=AF.Square, accum_out=ss.t[:, 0:1]),
                             reads=[xt.res], writes=[junk.res, ss.res])
                        yield
                        S.op("dve", lambda e: e.tensor_scalar(out=ss.t[:, 1:2], in0=ss.t[:, 0:1], scalar1=1.0 / D, scalar2=EPS,
                                                              op0=ALU.mult, op1=ALU.add), reads=[ss.res], writes=[ss.res])
                        yield
                        S.op("act", lambda e: e.activation(out=ss.t[:, 2:3], in_=ss.t[:, 1:2], func=AF.Sqrt), reads=[ss.res], writes=[ss.res])
                        yield
                        S.op("dve", lambda e: e.reciprocal(out=ss.t[:, 3:4], in_=ss.t[:, 2:3]), reads=[ss.res], writes=[ss.res])
                        S.op("dve", lambda e: e.tensor_scalar(out=xt.t[:], in0=xt.t[:], scalar1=ss.t[:, 3:4], scalar2=None, op0=ALU.mult),
                             reads=[xt.res, ss.res], writes=[xt.res])
                        yield
                        for kb in range(4):
                            pt = next_ps()
                            for kk in range(4):
                                k = kb * 4 + kk
                                S.op("pe", lambda e: e.transpose(out=pt.t[:, kk * 128:(kk + 1) * 128], in_=xt.t[:, k * 128:(k + 1) * 128],
                                                                 identity=ident.t[:]),
                                     reads=[xt.res, ident.res], writes=[pt.res], inc=(kk == 3))
                            yield
                            for kk in range(4):
                                k = kb * 4 + kk
                                o_ap = hT.t[:, k, i * 128:(i + 1) * 128]
                                i_ap = pt.t[:, kk * 128:(kk + 1) * 128]
                                sc = gsc.t[:, l * 16 + k, r:r + 1]
                                bi = mod.t[:, l * 48 + k, r:r + 1]
                                if kb % 2 == 0:
                                    S.op("act", lambda e: e.activation(out=o_ap, in_=i_ap, func=AF.Identity, bias=bi, scale=sc),
                                         reads=[pt.res, gsc.res, mod.res], dwrites=[hres[i]])
                                else:
                                    S.op("dve", lambda e: e.tensor_scalar(out=o_ap, in0=i_ap, scalar1=sc, scalar2=bi, op0=ALU.mult, op1=ALU.add),
                                         reads=[pt.res, gsc.res, mod.res], dwrites=[hres[i]])
                            yield

                    active = []
                    nxt = 0
                    rnd = 0
                    while nxt < NTILE or active:
                        if nxt < NTILE and len(active) < BD and rnd % 4 == 0:
                            active.append(b_tile(nxt))
                            nxt += 1
                        rnd += 1
                        for g_ in list(active):
                            try:
                                next(g_)
                            except StopIteration:
                                active.remove(g_)
                    if dbg and l == 0:
                        for k in range(16):
                            S.dma("sp", HT_dbg[k], hT.t[:, k, :], hT.res, reads=hres, dwrites=[dres["DBG"]])
                    S.barrier()
                if stop_after == "B":
                    return nc

                with ExitStack() as st:
                    wn = [2]

                    def load_w(c0, width):
                        if (c0, width) in preload:
                            return preload.pop((c0, width))
                        wb = wbuf[wn[0] % 3]
                        wn[0] += 1
                        S.dma("pool", wb.t[:, :, 0:width], w_in[l, :, c0:c0 + width].rearrange("(k p) n -> p k n", p=128),
                              wb.res, writes=[wb.res])
                        return wb

                    stg_b = [S.tile(st, "stgb%d" % i, [128, 512], BF16) for i in range(3)]
                    stg_f = [S.tile(st, "stgf%d" % i, [128, 512], F32) for i in range(3)]
                    sn = [0]

                    def nstg(fp32=False):
                        sn[0] += 1
                        return (stg_f if fp32 else stg_b)[sn[0] % 3]

                    def mm_tm(pt, i, wb, width):
                        for k in range(16):
                            S.op("pe", lambda e: e.matmul(pt.t[:, 0:width], lhsT=hT.t[:, k, i * 128:(i + 1) * 128], rhs=wb.t[:, k, 0:width],
                                                          start=(k == 0), stop=(k == 15)),
                                 reads=[hres[i], wb.res], writes=[pt.res], inc=(k == 15))

                    def mm_fm(pt, wb, m0, msz, t0, tsz):
                        tl = list(range(t0 // 128, (t0 + tsz) // 128))
                        for k in range(16):
                            S.op("pe", lambda e: e.matmul(pt.t[0:msz, 0:tsz], lhsT=wb.t[:, k, m0:m0 + msz], rhs=hT.t[:, k, t0:t0 + tsz],
                                                          start=(k == 0), stop=(k == 15)),
                                 reads=[hres[i] for i in tl] + [wb.res], writes=[pt.res], inc=(k == 15))

                    ecnt = [0]

                    def simple_tm(gname, dst, dname, func, fp32=False):
                        g0, gsz = _G[gname]
                        for c0 in range(0, gsz, 512):
                            wb = load_w(g0 + c0, 512)
                            for i in range(NTILE):
                                pt = next_ps()
                                mm_tm(pt, i, wb, 512)
                                sg = nstg(fp32)
                                ecnt[0] += 1
                                if func is not None:
                                    S.op("act", lambda e: e.activation(out=sg.t[:], in_=pt.t[:], func=func), reads=[pt.res], writes=[sg.res])
                                elif ecnt[0] % 2 == 0:
                                    S.op("act", lambda e: e.activation(out=sg.t[:], in_=pt.t[:], func=AF.Copy), reads=[pt.res], writes=[sg.res])
                                else:
                                    S.op("dve", lambda e: e.tensor_copy(out=sg.t[:], in_=pt.t[:]), reads=[pt.res], writes=[sg.res])
                                S.dma("sp", dst[i * 128:(i + 1) * 128, c0:c0 + 512], sg.t[:], sg.res, reads=[sg.res], dwrites=[dres[dname]])

                    def simple_fm(gname, dst, dname, func, scale=1.0, fp32=False, chunk0=0):
                        g0, gsz = _G[gname]
                        for c0 in range(0, gsz, 512):
                            wb = load_w(g0 + c0, 512)
                            for m in range(4):
                                ch = chunk0 + (c0 // 128) + m
                                for (t0, tsz) in TOK_GROUPS:
                                    pt = next_ps()
                                    mm_fm(pt, wb, m * 128, 128, t0, tsz)
                                    sg = nstg(fp32)
                                    ecnt[0] += 1
                                    if func is not None:
                                        S.op("act", lambda e: e.activation(out=sg.t[:, 0:tsz], in_=pt.t[:, 0:tsz], func=func, scale=scale),
                                             reads=[pt.res], writes=[sg.res])
                                    elif ecnt[0] % 2 == 0:
                                        S.op("act", lambda e: e.activation(out=sg.t[:, 0:tsz], in_=pt.t[:, 0:tsz], func=AF.Copy, scale=scale),
                                             reads=[pt.res], writes=[sg.res])
                                    else:
                                        S.op("dve", lambda e: e.tensor_scalar(out=sg.t[:, 0:tsz], in0=pt.t[:, 0:tsz], scalar1=scale, scalar2=None,
                                                                              op0=ALU.mult), reads=[pt.res], writes=[sg.res])
                                    S.dma("sp", dst[ch, :, t0:t0 + tsz], sg.t[:, 0:tsz], sg.res, reads=[sg.res], dwrites=[dres[dname]])

                    qk_stack = st
                    qk_tiles = {}

                    def qk_group(gname, dst, dname, gain_off):
                        g0, gsz = _G[gname]
                        DEPTH = 3
                        if True:
                            q = qk_stack
                            if not qk_tiles:
                                qk_tiles["sqs"] = [S.tile(q, "qk_sq%d" % i, [128, 512], F32) for i in range(DEPTH)]
                                qk_tiles["qns"] = [S.tile(q, "qk_qn%d" % i, [128, 512], F32) for i in range(DEPTH)]
                                qk_tiles["tas"] = [S.tile(q, "qk_ta%d" % i, [128, 8, 32], F32) for i in range(DEPTH)]
                                qk_tiles["tbs"] = [S.tile(q, "qk_tb%d" % i, [128, 8, 32], F32) for i in range(DEPTH)]
                                qk_tiles["tcs"] = [S.tile(q, "qk_tc%d" % i, [128, 8, 32], F32) for i in range(DEPTH)]
                                qk_tiles["tds"] = [S.tile(q, "qk_td%d" % i, [128, 8, 32], F32) for i in range(DEPTH)]
                                qk_tiles["qrs"] = [S.tile(q, "qk_qr%d" % i, [128, 512], BF16) for i in range(DEPTH)]
                                qk_tiles["st8s"] = [S.tile(q, "qk_s8%d" % i, [128, 16], F32) for i in range(DEPTH)]
                                qk_tiles["sTs"] = [S.tile(q, "qk_sT%d" % i, [128, 4, 128], BF16) for i in range(DEPTH)]
                            sqs, qns, tas, tbs, tcs, tds = (qk_tiles[k_] for k_ in ("sqs", "qns", "tas", "tbs", "tcs", "tds"))
                            qrs, st8s, sTs = (qk_tiles[k_] for k_ in ("qrs", "st8s", "sTs"))
                            gain = rowp.t[:, RP + gain_off:RP + gain_off + 64]
                            v3 = lambda ap: ap.rearrange("p (g d) -> p g d", g=8)

                            def qk_tile(c0, wb, i, n):
                                b_ = n % DEPTH
                                sq, qn, ta, tb, tc_, td, qo, st8, so = sqs[b_], qns[b_], tas[b_], tbs[b_], tcs[b_], tds[b_], qrs[b_], st8s[b_], sTs[b_]
                                pt = next_ps()
                                mm_tm(pt, i, wb, 512)
                                S.op("act", lambda e: e.activation(out=sq.t[:], in_=pt.t[:], func=AF.Square), reads=[pt.res], writes=[sq.res])
                                yield
                                S.op("dve", lambda e: e.reduce_sum(out=st8.t[:, 0:8], in_=v3(sq.t[:]), axis=AX.X), reads=[sq.res], writes=[st8.res])
                                S.op("dve", lambda e: e.tensor_scalar(out=st8.t[:, 0:8], in0=st8.t[:, 0:8], scalar1=1.0 / 64, scalar2=EPS,
                                                                      op0=ALU.mult, op1=ALU.add), reads=[st8.res], writes=[st8.res])
                                yield
                                S.op("act", lambda e: e.activation(out=st8.t[:, 0:8], in_=st8.t[:, 0:8], func=AF.Sqrt), reads=[st8.res], writes=[st8.res])
                                yield
                                S.op("dve", lambda e: e.reciprocal(out=st8.t[:, 8:16], in_=st8.t[:, 0:8]), reads=[st8.res], writes=[st8.res])
                                S.op("dve", lambda e: e.tensor_tensor(out=v3(qn.t[:]), in0=v3(pt.t[:]),
                                                                      in1=st8.t[:, 8:16].unsqueeze(2).to_broadcast([128, 8, 64]), op=ALU.mult),
                                     reads=[pt.res, st8.res], writes=[qn.res])
                                yield
                                if i < 2:
                                    S.op("pool", lambda e: e.tensor_tensor(out=v3(qo.t[:]), in0=v3(qn.t[:]),
                                                                           in1=gain.unsqueeze(1).to_broadcast([128, 8, 64]), op=ALU.mult),
                                         reads=[qn.res, rowp.res], writes=[qo.res])
                                    yield
                                else:
                                    S.op("pool", lambda e: e.tensor_tensor(out=v3(qn.t[:]), in0=v3(qn.t[:]),
                                                                           in1=gain.unsqueeze(1).to_broadcast([128, 8, 64]), op=ALU.mult),
                                         reads=[qn.res, rowp.res], writes=[qn.res])
                                    yield
                                    pi = i - 2
                                    cosb = rope.t[:, pi, :].unsqueeze(1).to_broadcast([128, 8, 32])
                                    sinb = rope.t[:, 16 + pi, :].unsqueeze(1).to_broadcast([128, 8, 32])
                                    q1 = v3(qn.t[:])[:, :, 0:32]
                                    q2 = v3(qn.t[:])[:, :, 32:64]
                                    o1 = v3(qo.t[:])[:, :, 0:32]
                                    o2 = v3(qo.t[:])[:, :, 32:64]
                                    S.op("dve", lambda e: e.tensor_tensor(out=ta.t[:], in0=q1, in1=cosb, op=ALU.mult), reads=[qn.res, rope.res], writes=[ta.res])
                                    S.op("pool", lambda e: e.tensor_tensor(out=tb.t[:], in0=q2, in1=sinb, op=ALU.mult), reads=[qn.res, rope.res], writes=[tb.res])
                                    S.op("dve", lambda e: e.tensor_tensor(out=tc_.t[:], in0=q2, in1=cosb, op=ALU.mult), reads=[qn.res, rope.res], writes=[tc_.res])
                                    S.op("pool", lambda e: e.tensor_tensor(out=td.t[:], in0=q1, in1=sinb, op=ALU.mult), reads=[qn.res, rope.res], writes=[td.res])
                                    yield
                                    S.op("dve", lambda e: e.tensor_tensor(out=o1, in0=ta.t[:], in1=tb.t[:], op=ALU.subtract),
                                         reads=[ta.res, tb.res], writes=[qo.res])
                                    S.op("dve", lambda e: e.tensor_tensor(out=o2, in0=tc_.t[:], in1=td.t[:], op=ALU.add),
                                         reads=[tc_.res, td.res], writes=[qo.res])
                                    yield
                                p2 = next_ps()
                                pb = p2.t[:].bitcast(BF16)
                                for hh in range(4):
                                    S.op("pe", lambda e: e.transpose(out=pb[:, hh * 128:(hh + 1) * 128], in_=qo.t[:, hh * 128:(hh + 1) * 128],
                                                                     identity=identb.t[:]),
                                         reads=[qo.res, identb.res], writes=[p2.res], inc=(hh == 3))
                                yield
                                S.op("act", lambda e: e.activation(out=so.t[:].rearrange("p a b -> p (a b)"), in_=pb[:, 0:512], func=AF.Copy),
                                     reads=[p2.res], writes=[so.res])
                                h0 = c0 // 128
                                S.dma("sp", dst[h0:h0 + 4, :, i * 128:(i + 1) * 128].rearrange("h p t -> p h t"), so.t[:], so.res,
                                      reads=[so.res], dwrites=[dres[dname]])

                            work = []
                            for c0 in range(0, gsz, 512):
                                for i in range(NTILE):
                                    work.append((c0, i))
                            wbs = {}
                            active = []
                            nxt = 0
                            rnd = 0
                            while nxt < len(work) or active:
                                if nxt < len(work) and len(active) < DEPTH and rnd % 3 == 0:
                                    c0, i = work[nxt]
                                    if c0 not in wbs:
                                        wbs[c0] = load_w(g0 + c0, 512)
                                    active.append(qk_tile(c0, wbs[c0], i, nxt))
                                    nxt += 1
                                rnd += 1
                                for g_ in list(active):
                                    try:
                                        next(g_)
                                    except StopIteration:
                                        active.remove(g_)

                    simple_tm("pu", U_d, "U", None)
                    simple_fm("pg", SPG_d, "SPG", AF.Silu)
                    qk_group("dq", QT_d, "QT", 0)
                    qk_group("dk", KT_d, "KT", 64)
                    simple_tm("dv", V_d, "V", None)
                    simple_tm("dg", SDG_d, "SDG", AF.Silu)
                    simple_fm("gq", GQT_d, "GQT", None, scale=128.0 ** -0.5, fp32=True)
                    simple_fm("gk", GKT_d, "GKT", None, fp32=True)
                    simple_tm("gk", GK_d, "GK", None, fp32=True)
                    simple_tm("gv", GV_d, "GV", None)
                    simple_tm("gg", SGG_d, "SGG", AF.Silu)
                    g0 = _G["glf"][0]
                    wb = load_w(g0, 32)
                    for dd in range(2):
                        for (t0, tsz) in TOK_GROUPS:
                            pt = next_ps()
                            mm_fm(pt, wb, dd * 16, 16, t0, tsz)
                            sg = nstg(True)
                            S.op("dve", lambda e: e.tensor_copy(out=sg.t[0:16, 0:tsz], in_=pt.t[0:16, 0:tsz]), reads=[pt.res], writes=[sg.res])
                            S.dma("sp", LRT_d[dd, :, t0:t0 + tsz], sg.t[0:16, 0:tsz], sg.res, reads=[sg.res], dwrites=[dres["LRT"]])
                    simple_fm("mg", MGT_d, "MGT", AF.Sigmoid)
                    S.barrier()
            if stop_after == "C":
                return nc

            with ExitStack() as st:
                Ut = S.tile(st, "Ut", [128, NTILE, 1024], BF16)
                pw = S.tile(st, "pw", [128, 8, 256], BF16)
                dT = [S.tile(st, "dT%d" % i, [128, 2, NT], BF16) for i in range(2)]
                spg = [S.tile(st, "spg%d" % i, [128, 512], BF16) for i in range(3)]
                po = [S.tile(st, "po%d" % i, [128, 512], BF16) for i in range(3)]
                for i0 in range(0, NTILE, 6):
                    S.dma("sp", Ut.t[:, i0:i0 + 6, :], U_d[i0 * 128:(i0 + 6) * 128, :].rearrange("(i p) c -> p i c", p=128), Ut.res,
                          reads=[dres["U"]], dwrites=[Ut.res])
                S.dma("pool", pw.t[:], pool_w[l].rearrange("g (cc p) d -> p (g cc) d", p=128), pw.res, writes=[pw.res])
                nn = 0
                for g in range(4):
                    dt_ = dT[g % 2]
                    for cc in range(2):
                        cb = g * 2 + cc
                        for (t0, tsz) in TOK_GROUPS:
                            pt = next_ps()
                            tiles = list(range(t0 // 128, (t0 + tsz) // 128))
                            for ti, i in enumerate(tiles):
                                first = i in (0, 2)
                                last = i in (1, NTILE - 1)
                                contrib = [(i, 3 if first else (4 if last else 1))]
                                if not first:
                                    contrib.append((i - 1, 0))
                                if not last:
                                    contrib.append((i + 1, 2))
                                for ci, (j, var) in enumerate(contrib):
                                    S.op("pe", lambda e: e.matmul(pt.t[:, ti * 128:(ti + 1) * 128], lhsT=Ut.t[:, j, cb * 128:(cb + 1) * 128],
                                                                  rhs=band.t[:, g * 5 + var, :], start=(ci == 0), stop=(ci == len(contrib) - 1)),
                                         reads=[Ut.res, band.res], writes=[pt.res], inc=(ti == len(tiles) - 1 and ci == len(contrib) - 1))
                            nn += 1
                            if nn % 2 == 0:
                                S.op("act", lambda e: e.activation(out=dt_.t[:, cc, t0:t0 + tsz], in_=pt.t[:, 0:tsz], func=AF.Copy),
                                     reads=[pt.res], dwrites=[dt_.res])
                            else:
                                S.op("dve", lambda e: e.tensor_copy(out=dt_.t[:, cc, t0:t0 + tsz], in_=pt.t[:, 0:tsz]),
                                     reads=[pt.res], dwrites=[dt_.res])
                    d2 = [(db, t0, tsz) for db in range(2) for (t0, tsz) in TOK_GROUPS]

                    def spg_load(m):
                        db, t0, tsz = d2[m]
                        S.dma("sp", spg[m % 3].t[:, 0:tsz], SPG_d[g * 2 + db, :, t0:t0 + tsz], spg[m % 3].res, reads=[dres["SPG"]], writes=[spg[m % 3].res])
                    spg_load(0)
                    spg_load(1)
                    for m, (db, t0, tsz) in enumerate(d2):
                        ch = g * 2 + db
                        if m + 2 < len(d2):
                            spg_load(m + 2)
                        pt = next_ps()
                        for cc in range(2):
                            S.op("pe", lambda e: e.matmul(pt.t[:, 0:tsz], lhsT=pw.t[:, g * 2 + cc, db * 128:(db + 1) * 128],
                                                          rhs=dt_.t[:, cc, t0:t0 + tsz], start=(cc == 0), stop=(cc == 1)),
                                 reads=[pw.res, dt_.res], writes=[pt.res], inc=(cc == 1))
                        sp_ = spg[m % 3]
                        po_ = po[m % 3]
                        S.op("dve", lambda e: e.scalar_tensor_tensor(out=po_.t[:, 0:tsz], in0=pt.t[:, 0:tsz],
                                                                     scalar=colp.t[:, l * 72 + 64 + ch:l * 72 + 65 + ch],
                                                                     in1=sp_.t[:, 0:tsz], op0=ALU.mult, op1=ALU.mult),
                             reads=[pt.res, colp.res, sp_.res], writes=[po_.res])
                        S.dma("sp", POT_d[ch, :, t0:t0 + tsz], po_.t[:, 0:tsz], po_.res, reads=[po_.res], dwrites=[dres["POT"]])
                    dt_.res.w = {}
                S.barrier()
            if stop_after == "D":
                return nc

            with ExitStack() as st:
                KT = [S.tile(st, "KT%d" % i, [128, NT], BF16) for i in range(2)]
                QT = [S.tile(st, "QT%d" % i, [128, NT], BF16) for i in range(2)]
                Vx = [S.tile(st, "Vx%d" % i, [128, NTILE, 130], BF16) for i in range(2)]
                ptl = [S.tile(st, "ptl%d" % i, [128, 512], BF16) for i in range(3)]
                lam = S.tile(st, "lam", [128, 8], F32)
                lt = S.tile(st, "lamt", [128, 64], F32)
                subs = S.tile(st, "subs", [128, 128], F32)
                accs = [S.tile(st, "accs%d" % i, [128, 4, 2, 130], F32) for i in range(2)]
                rz = [S.tile(st, "rz%d" % i, [128, 4, 8], F32) for i in range(2)]
                t0_ = [S.tile(st, "t0_%d" % i, [128, 4, 128], F32) for i in range(2)]
                t1_ = [S.tile(st, "t1_%d" % i, [128, 4, 128], F32) for i in range(2)]
                sqt2 = [S.tile(st, "esq%d" % i, [128, 4, 128], F32) for i in range(2)]
                ob = [S.tile(st, "ob%d" % i, [128, 4, 128], BF16) for i in range(2)]
                sdg = [S.tile(st, "sdg%d" % i, [128, 4, 128], BF16) for i in range(2)]
                oT = [S.tile(st, "oT%d" % i, [128, 512], BF16) for i in range(2)]
                for a in range(2):
                    S.op("dve", lambda e: e.tensor_tensor(out=lt.t[:], in0=rowp.t[:, RP + 512 + a * 128:RP + 576 + a * 128],
                                                          in1=rowp.t[:, RP + 576 + a * 128:RP + 640 + a * 128], op=ALU.mult),
                         reads=[rowp.res], writes=[lt.res])
                    S.op("dve", lambda e: e.reduce_sum(out=lam.t[:, a:a + 1], in_=lt.t[:], axis=AX.X), reads=[lt.res], writes=[lam.res])
                S.op("act", lambda e: e.activation(out=lam.t[:, 2:4], in_=lam.t[:, 0:2], func=AF.Exp), reads=[lam.res], writes=[lam.res])
                S.op("dve", lambda e: e.tensor_tensor(out=lam.t[:, 4:5], in0=lam.t[:, 3:4], in1=lam.t[:, 2:3], op=ALU.subtract),
                     reads=[lam.res], writes=[lam.res])
                S.op("dve", lambda e: e.tensor_scalar(out=lam.t[:, 5:6], in0=lam.t[:, 4:5], scalar1=-lam_init, scalar2=None, op0=ALU.add),
                     reads=[lam.res], writes=[lam.res])
                S.op("act", lambda e: e.activation(out=subs.t[:], in_=rowp.t[:, RP + 128:RP + 256], func=AF.Copy, scale=1.0 - lam_init),
                     reads=[rowp.res], writes=[subs.res])
                for hb in range(2):
                    S.op("dve", lambda e: e.memset(Vx[hb].t[:, :, 128:130], 1.0), writes=[Vx[hb].res])

                accb = ps[0:4]
                accv = psall[:, 0:2048].rearrange("p (q j c) -> p q j c", q=4, j=2)
                ptl2 = [S.tile(st, "ptl2_%d" % i, [128, 2, 512], BF16) for i in range(2)]
                its = []
                for h in range(8):
                    qgroups = [(256 + 512 * a, 512, list(range(NTILE))) for a in range(4)]
                    if need_ctx:
                        qgroups = [(0, 256, [0, 1])] + qgroups
                    for (q0, qsz, ktiles) in qgroups:
                        for kti, kt in enumerate(ktiles):
                            its.append(dict(h=h, q0=q0, qsz=qsz, kt=kt, first=(kti == 0), last=(kti == len(ktiles) - 1)))
                loaded = set()
                hstart = {}
                fin = [0]

                def load_head(h):
                    loaded.add(h)
                    kt_, qt_, vx_ = KT[h % 2], QT[h % 2], Vx[h % 2]
                    S.dma("sp", kt_.t[:], KT_d[h], kt_.res, reads=[dres["KT"]], writes=[kt_.res])
                    S.dma("sp", qt_.t[:], QT_d[h], qt_.res, reads=[dres["QT"]], writes=[qt_.res])
                    S.dma("sp", vx_.t[:, :, 0:128], V_d[:, h * 128:(h + 1) * 128].rearrange("(i p) e -> p i e", p=128), vx_.res,
                          reads=[dres["V"]], dwrites=[vx_.res])

                def emit_S(n):
                    it = its[n]
                    h = it["h"]
                    kt_, qt_, vx_ = KT[h % 2], QT[h % 2], Vx[h % 2]
                    if h not in hstart:
                        hstart[h] = n
                    if h not in loaded:
                        load_head(h)
                    if h + 1 < 8 and (h + 1) not in loaded and n - hstart[h] >= 6:
                        load_head(h + 1)
                    kt, q0, qsz = it["kt"], it["q0"], it["qsz"]
                    sb = n % 2
                    pa, pb_ = ps[4 + 2 * sb], ps[5 + 2 * sb]
                    pt_ = ptl2[sb]
                    for j, p_s in enumerate((pa, pb_)):
                        S.op("pe", lambda e: e.matmul(p_s.t[:, 0:qsz], lhsT=kt_.t[j * 64:(j + 1) * 64, kt * 128:(kt + 1) * 128],
                                                      rhs=qt_.t[j * 64:(j + 1) * 64, q0:q0 + qsz], start=True, stop=True),
                             reads=[kt_.res, qt_.res], writes=[p_s.res], inc=(j == 1))
                    sview = psall[:, 2048 + sb * 1024:2048 + (sb + 1) * 1024].rearrange("p (j c) -> p j c", j=2)
                    S.op("act", lambda e: e.activation(out=pt_.t[:, :, 0:qsz], in_=sview[:, :, 0:qsz], func=AF.Exp, scale=0.125),
                         reads=[pa.res, pb_.res], writes=[pt_.res])

                def emit_PV(n):
                    it = its[n]
                    h = it["h"]
                    vx_ = Vx[h % 2]
                    kt, q0, qsz = it["kt"], it["q0"], it["qsz"]
                    nq = qsz // 128
                    pt_ = ptl2[n % 2]
                    for j in range(2):
                        for qi in range(nq):
                            S.op("pe", lambda e: e.matmul(accb[qi].t[:, j * 256:j * 256 + 130], lhsT=pt_.t[:, j, qi * 128:(qi + 1) * 128],
                                                          rhs=vx_.t[:, kt, :], start=(it["first"] and j == 0), stop=(it["last"] and j == 1),
                                                          skip_group_check=True),
                                 reads=[pt_.res, vx_.res], writes=[accb[qi].res], inc=(j == 1 and qi == nq - 1))
                    if not it["last"]:
                        return
                    f = fin[0] % 2
                    fin[0] += 1
                    ac, rz_, a0_, a1_, ob_, sdg_ = accs[f], rz[f], t0_[f], t1_[f], ob[f], sdg[f]
                    sqt = sqt2[f]
                    bc = lambda ap: ap.unsqueeze(2).to_broadcast([128, nq, 128])
                    S.dma("sp", sdg_.t[:, 0:nq, :], SDG_d[q0:q0 + qsz, h * 128:(h + 1) * 128].rearrange("(i p) e -> p i e", p=128), sdg_.res,
                          reads=[dres["SDG"]], writes=[sdg_.res])
                    S.op("dve", lambda e: e.tensor_copy(out=ac.t[:, 0:nq], in_=accv[:, 0:nq, :, 0:130]),
                         reads=[accb[qi].res for qi in range(nq)], writes=[ac.res])
                    S.op("dve", lambda e: e.reciprocal(out=rz_.t[:, 0:nq, 0:2], in_=ac.t[:, 0:nq, :, 128]), reads=[ac.res], writes=[rz_.res])
                    S.op("dve", lambda e: e.tensor_scalar(out=rz_.t[:, 0:nq, 1], in0=rz_.t[:, 0:nq, 1], scalar1=lam.t[:, 5:6], scalar2=None, op0=ALU.mult),
                         reads=[rz_.res, lam.res], writes=[rz_.res])
                    S.op("pool", lambda e: e.tensor_tensor(out=a1_.t[:, 0:nq], in0=ac.t[:, 0:nq, 1, 0:128], in1=bc(rz_.t[:, 0:nq, 1]), op=ALU.mult),
                         reads=[ac.res, rz_.res], writes=[a1_.res])
                    S.op("dve", lambda e: e.tensor_tensor(out=a0_.t[:, 0:nq], in0=ac.t[:, 0:nq, 0, 0:128], in1=bc(rz_.t[:, 0:nq, 0]), op=ALU.mult),
                         reads=[ac.res, rz_.res], writes=[a0_.res])
                    S.op("pool", lambda e: e.tensor_tensor(out=a0_.t[:, 0:nq], in0=a0_.t[:, 0:nq], in1=a1_.t[:, 0:nq], op=ALU.add),
                         reads=[a0_.res, a1_.res], writes=[a0_.res])
                    S.op("pool", lambda e: e.tensor_tensor(out=sqt.t[:, 0:nq], in0=a0_.t[:, 0:nq], in1=a0_.t[:, 0:nq], op=ALU.mult),
                         reads=[a0_.res], writes=[sqt.res])

                    def fin_b():
                        S.op("dve", lambda e: e.reduce_sum(out=rz_.t[:, 0:nq, 2], in_=sqt.t[:, 0:nq], axis=AX.X), reads=[sqt.res], writes=[rz_.res])
                        S.op("dve", lambda e: e.tensor_scalar(out=rz_.t[:, 0:nq, 3], in0=rz_.t[:, 0:nq, 2], scalar1=1.0 / 128, scalar2=EPS,
                                                              op0=ALU.mult, op1=ALU.add), reads=[rz_.res], writes=[rz_.res])
                        S.op("act", lambda e: e.activation(out=rz_.t[:, 0:nq, 4], in_=rz_.t[:, 0:nq, 3], func=AF.Ln), reads=[rz_.res], writes=[rz_.res])
                        S.op("act", lambda e: e.activation(out=rz_.t[:, 0:nq, 5], in_=rz_.t[:, 0:nq, 4], func=AF.Exp, scale=-0.5), reads=[rz_.res], writes=[rz_.res])

                    def fin_c():
                        S.op("dve", lambda e: e.tensor_tensor(out=a0_.t[:, 0:nq], in0=a0_.t[:, 0:nq], in1=bc(rz_.t[:, 0:nq, 5]), op=ALU.mult),
                             reads=[a0_.res, rz_.res], writes=[a0_.res])
                        S.op("pool", lambda e: e.tensor_tensor(out=a0_.t[:, 0:nq], in0=a0_.t[:, 0:nq],
                                                               in1=subs.t[:].unsqueeze(1).to_broadcast([128, nq, 128]), op=ALU.mult),
                             reads=[a0_.res, subs.res], writes=[a0_.res])
                        S.op("pool", lambda e: e.tensor_tensor(out=ob_.t[:, 0:nq], in0=a0_.t[:, 0:nq], in1=sdg_.t[:, 0:nq], op=ALU.mult),
                             reads=[a0_.res, sdg_.res], writes=[ob_.res])
                        S.dma("sp", DOtm_d[q0:q0 + qsz, h * 128:(h + 1) * 128].rearrange("(i p) e -> p i e", p=128), ob_.t[:, 0:nq, :], ob_.res,
                              reads=[ob_.res], dwrites=[dres["DOTM"]])
                    deferred.append((n + 4, fin_b))
                    deferred.append((n + 7, fin_c))

                deferred = []
                LOOK = 1
                for n in range(len(its) + LOOK):
                    if n < len(its):
                        emit_S(n)
                    if n >= LOOK:
                        emit_PV(n - LOOK)
                    while deferred and deferred[0][0] <= n - LOOK:
                        deferred.pop(0)[1]()
                while deferred:
                    deferred.pop(0)[1]()
                S.barrier()
            with ExitStack() as st:
                dtm = [S.tile(st, "dtm%d" % i, [128, 1024], BF16) for i in range(3)]
                dTt = [S.tile(st, "dTt%d" % i, [128, 8, 128], BF16) for i in range(3)]
                e2_tiles = list(range(0 if need_ctx else 2, NTILE))

                def e2_load(m):
                    i = e2_tiles[m]
                    S.dma("sp", dtm[m % 3].t[:], DOtm_d[i * 128:(i + 1) * 128, :], dtm[m % 3].res, reads=[dres["DOTM"]], writes=[dtm[m % 3].res])
                e2_load(0)
                e2_load(1)
                for m, i in enumerate(e2_tiles):
                    if m + 2 < len(e2_tiles):
                        e2_load(m + 2)
                    d_, dT_ = dtm[m % 3], dTt[m % 3]
                    p2 = next_ps()
                    pb = p2.t[:].bitcast(BF16)
                    for cb in range(8):
                        S.op("pe", lambda e: e.transpose(out=pb[:, cb * 128:(cb + 1) * 128], in_=d_.t[:, cb * 128:(cb + 1) * 128], identity=identb.t[:]),
                             reads=[d_.res, identb.res], writes=[p2.res], inc=(cb == 7))
                    S.op("act", lambda e: e.activation(out=dT_.t[:].rearrange("p a b -> p (a b)"), in_=pb[:, 0:1024], func=AF.Copy),
                         reads=[p2.res], writes=[dT_.res])
                    S.dma("sp", DOT_d[:, :, i * 128:(i + 1) * 128].rearrange("c p t -> p c t"), dT_.t[:], dT_.res, reads=[dT_.res], dwrites=[dres["DOT"]])
                S.barrier()
            if stop_after == "E":
                return nc

            with ExitStack() as st:
                w2 = S.tile(st, "w2", [17, 2, 512], F32)
                Sf = [[S.tile(st, "Sf%d%d" % (d, hh), [128, 256], F32) for hh in range(4)] for d in range(2)]
                tri2 = S.tile(st, "tri2", [128, 6, 128], F32)
                S.dma("sp", tri2.t[:].rearrange("p a b -> p (a b)"), tri2_d, tri2.res, writes=[tri2.res])
                S.dma("sp", w2.t[:], w2_d[l].rearrange("d k n -> k d n"), w2.res, writes=[w2.res])
                for d in range(2):
                    for hh in range(4):
                        S.op("dve", lambda e: e.memset(Sf[d][hh].t[:], 0.0), writes=[Sf[d][hh].res])
                NB = 2

                def mk(name, shape, dt, n=NB):
                    return [[S.tile(st, "%s%d_%d" % (name, d, i), shape, dt) for i in range(n)] for d in range(2)]
                lr = mk("lr", [32, 128], F32)
                gk = mk("gk", [128, 512], F32)
                gv = mk("gv", [128, 1024], BF16)
                gqT = mk("gqT", [128, 4, 128], F32)
                gkT = mk("gkT", [128, 4, 128], F32)
                e1 = mk("e1", [128, 512], F32, 1)
                g1 = mk("g1", [128, 512], F32)
                eblt = mk("eblt", [128, 512], F32, 1)
                kps = mk("kps", [128, 512], BF16)
                eb = mk("eb", [128, 512], F32)
                enT = mk("enT", [128, 512], F32, 1)
                qpT = mk("qpT", [128, 4, 128], BF16)
                kpT = mk("kpT", [128, 4, 128], BF16)
                qpad = mk("qpad", [128, 4, 2, 128], BF16)
                atb = mk("atb", [128, 4, 128], BF16)
                osb = mk("osb", [128, 1024], F32)
                Sb2 = [[[S.tile(st, "Sb%d%d_%d" % (d, hh, v), [128, 256], BF16) for v in range(2)] for hh in range(4)] for d in range(2)]
                sver = [[0] * 4 for _ in range(2)]
                for d in range(2):
                    for i in range(NB):
                        S.op("dve", lambda e: e.memset(lr[d][i].t[:], 1.0), writes=[lr[d][i].res])
                        S.op("pool", lambda e: e.memset(qpad[d][i].t[:], 0.0), writes=[qpad[d][i].res])
                    for hh in range(4):
                        for v in range(2):
                            S.op("pool", lambda e: e.memset(Sb2[d][hh][v].t[:], 0.0), writes=[Sb2[d][hh][v].res])
                fwd = list(range(NTILE))
                bwd = [1, 0] + list(range(NTILE - 1, 1, -1))
                pc = [0]

                def gps():
                    pc[0] += 1
                    return ps[pc[0] % 8]

                def gla_load(d, i, n):
                    b = n % NB
                    t0 = i * 128
                    lr_, gk_, gv_, gqT_, gkT_ = lr[d][b], gk[d][b], gv[d][b], gqT[d][b], gkT[d][b]
                    S.dma("sp", lr_.t[0:16, :], LRT_d[d, :, t0:t0 + 128], lr_.res, reads=[dres["LRT"]], writes=[lr_.res])
                    S.dma("sp", gk_.t[:], GK_d[t0:t0 + 128, :], gk_.res, reads=[dres["GK"]], writes=[gk_.res])
                    S.dma("sp", gv_.t[:], GV_d[t0:t0 + 128, :], gv_.res, reads=[dres["GV"]], writes=[gv_.res])
                    S.dma("sp", gqT_.t[:], GQT_d[:, :, t0:t0 + 128].rearrange("h p t -> p h t"), gqT_.res, reads=[dres["GQT"]], writes=[gqT_.res])
                    S.dma("sp", gkT_.t[:], GKT_d[:, :, t0:t0 + 128].rearrange("h p t -> p h t"), gkT_.res, reads=[dres["GKT"]], writes=[gkT_.res])

                def gla_tile(d, i, n):
                    b = n % NB
                    t0 = i * 128
                    bank = [ps[4 * d + q] for q in range(4)]
                    lr_, gk_, gv_, gqT_, gkT_ = lr[d][b], gk[d][b], gv[d][b], gqT[d][b], gkT[d][b]
                    e1_, g1_, eblt_, kps_, eb_, enT_ = e1[d][0], g1[d][b], eblt[d][0], kps[d][b], eb[d][b], enT[d][0]
                    qpT_, kpT_, qpad_, atb_, osb_ = qpT[d][b], kpT[d][b], qpad[d][b], atb[d][b], osb[d][b]
                    pz = bank[0]
                    S.op("pe", lambda e: e.matmul(pz.t[:, :], lhsT=lr_.t[0:17, :], rhs=w2.t[:, d, :], start=True, stop=True),
                         reads=[lr_.res, w2.res], writes=[pz.res])
                    S.op("act", lambda e: e.activation(out=e1_.t[:], in_=pz.t[:, :], func=AF.Exp, scale=-1.0), reads=[pz.res], writes=[e1_.res])
                    S.op("act", lambda e: e.activation(out=g1_.t[:], in_=e1_.t[:], func=AF.Ln, bias=1.0), reads=[e1_.res], writes=[g1_.res])
                    yield
                    pbl = bank[1]
                    S.op("pe", lambda e: e.matmul(pbl.t[:, :], lhsT=tri2.t[:, 4 + d, :], rhs=g1_.t[:], start=True, stop=True),
                         reads=[tri2.res, g1_.res], writes=[pbl.res])
                    S.op("act", lambda e: e.activation(out=eblt_.t[:], in_=pbl.t[:, :], func=AF.Exp), reads=[pbl.res], writes=[eblt_.res])
                    S.op("dve", lambda e: e.tensor_tensor(out=kps_.t[:], in0=gk_.t[:], in1=eblt_.t[:], op=ALU.mult),
                         reads=[gk_.res, eblt_.res], writes=[kps_.res])
                    yield
                    pbT = bank[2]
                    for hh in range(4):
                        S.op("pe", lambda e: e.matmul(pbT.t[:, hh * 128:(hh + 1) * 128], lhsT=g1_.t[:, hh * 128:(hh + 1) * 128], rhs=tri2.t[:, 2 + d, :],
                                                      start=True, stop=True),
                             reads=[g1_.res, tri2.res], writes=[pbT.res], inc=(hh == 3))
                    S.op("act", lambda e: e.activation(out=eb_.t[:], in_=pbT.t[:, :], func=AF.Exp), reads=[pbT.res], writes=[eb_.res])
                    S.op("act", lambda e: e.activation(out=enT_.t[:], in_=pbT.t[:, :], func=AF.Exp, scale=-1.0), reads=[pbT.res], writes=[enT_.res])
                    yield
                    f2 = lambda ap: ap.rearrange("p a b -> p (a b)")
                    S.op("dve", lambda e: e.tensor_tensor(out=f2(qpT_.t[:]), in0=f2(gqT_.t[:]), in1=eb_.t[:], op=ALU.mult),
                         reads=[gqT_.res, eb_.res], writes=[qpT_.res])
                    S.op("pool", lambda e: e.tensor_tensor(out=f2(kpT_.t[:]), in0=f2(gkT_.t[:]), in1=enT_.t[:], op=ALU.mult),
                         reads=[gkT_.res, enT_.res], writes=[kpT_.res])
                    for c in range(2):
                        S.op("pool", lambda e: e.tensor_copy(out=qpad_.t[:, :, c, c * 64:(c + 1) * 64], in_=qpT_.t[:, :, c * 64:(c + 1) * 64]),
                             reads=[qpT_.res], writes=[qpad_.res])
                    yield
                    pat = bank[3]
                    for hh in range(4):
                        S.op("pe", lambda e: e.matmul(pat.t[:, hh * 128:(hh + 1) * 128], lhsT=kpT_.t[:, hh, :], rhs=qpT_.t[:, hh, :], start=True, stop=True),
                             reads=[kpT_.res, qpT_.res], writes=[pat.res], inc=(hh == 3))
                    S.op("dve", lambda e: e.tensor_tensor(out=atb_.t[:], in0=pat.t[:, :].rearrange("p (a b) -> p a b", a=4),
                                                          in1=tri2.t[:, d, :].unsqueeze(1).to_broadcast([128, 4, 128]), op=ALU.mult),
                         reads=[pat.res, tri2.res], writes=[atb_.res])
                    yield
                    corder = (0, 1) if d == 0 else (1, 0)
                    pkv = {}

                    def kv_mm(c):
                        for hp in range(2):
                            pk = bank[hp]
                            for h2 in range(2):
                                hh = hp * 2 + h2
                                pkv[(c, hh)] = (pk, h2)
                                S.op("pe", lambda e: e.matmul(pk.t[:, h2 * 256:(h2 + 1) * 256], lhsT=kps_.t[c * 64:(c + 1) * 64, hh * 128:(hh + 1) * 128],
                                                              rhs=gv_.t[c * 64:(c + 1) * 64, hh * 256:(hh + 1) * 256], start=True, stop=True),
                                     reads=[kps_.res, gv_.res], writes=[pk.res], inc=(h2 == 1))
                    kv_mm(corder[0])
                    yield
                    po = [bank[2], bank[3]]
                    c0_, c1_ = corder
                    for hh in range(4):
                        pb_, reg = po[hh // 2], (hh % 2) * 256
                        S.op("pe", lambda e: e.matmul(pb_.t[:, reg:reg + 256], lhsT=qpad_.t[:, hh, c0_, :], rhs=Sb2[d][hh][sver[d][hh]].t[:],
                                                      start=(hh % 2 == 0), stop=False, skip_group_check=True),
                             reads=[qpad_.res, Sb2[d][hh][sver[d][hh]].res], writes=[pb_.res], inc=False)
                        S.op("pe", lambda e: e.matmul(pb_.t[:, reg:reg + 256], lhsT=atb_.t[:, hh, :], rhs=gv_.t[:, hh * 256:(hh + 1) * 256],
                                                      start=False, stop=False, skip_group_check=True),
                             reads=[atb_.res, gv_.res], writes=[pb_.res], inc=(hh == 3))
                    yield
                    for ci, c in enumerate(corder):
                        col = (c * 64 + 63) if d == 0 else (c * 64)
                        for hh in range(4):
                            pk, h2 = pkv[(c, hh)]
                            ebl = eb_.t[:, hh * 128 + col:hh * 128 + col + 1]
                            S.op("dve", lambda e: e.scalar_tensor_tensor(out=Sf[d][hh].t[:], in0=Sf[d][hh].t[:], scalar=ebl, in1=pk.t[:, h2 * 256:(h2 + 1) * 256],
                                                                         op0=ALU.mult, op1=ALU.add),
                                 reads=[Sf[d][hh].res, eb_.res, pk.res], writes=[Sf[d][hh].res])
                            nv = 1 - sver[d][hh]
                            S.op("act", lambda e: e.activation(out=Sb2[d][hh][nv].t[:], in_=Sf[d][hh].t[:], func=AF.Copy), reads=[Sf[d][hh].res], writes=[Sb2[d][hh][nv].res])
                            sver[d][hh] = nv
                        yield
                        if ci == 0:
                            kv_mm(corder[1])
                            for hh in range(4):
                                pb_, reg = po[hh // 2], (hh % 2) * 256
                                S.op("pe", lambda e: e.matmul(pb_.t[:, reg:reg + 256], lhsT=qpad_.t[:, hh, c1_, :], rhs=Sb2[d][hh][sver[d][hh]].t[:],
                                                              start=False, stop=True, skip_group_check=True),
                                     reads=[qpad_.res, Sb2[d][hh][sver[d][hh]].res], writes=[pb_.res], inc=(hh == 3))
                    yield
                    S.op("act", lambda e: e.activation(out=osb_.t[:, 0:512], in_=po[0].t[:, :], func=AF.Copy), reads=[po[0].res], writes=[osb_.res])
                    S.op("dve", lambda e: e.tensor_copy(out=osb_.t[:, 512:1024], in_=po[1].t[:, :]), reads=[po[1].res], writes=[osb_.res])
                    S.dma("sp", O_d[d][t0:t0 + 128, :], osb_.t[:], osb_.res, reads=[osb_.res], dwrites=[dres["OF" if d == 0 else "OB"]])

                gla_load(0, fwd[0], 0)
                gla_load(1, bwd[0], 0)
                do_ada = (l + 1 < n_layers)
                if do_ada:
                    wab2 = [S.tile(st, "wab2_%d" % i, [128, 16, 512], BF16) for i in range(2)]
                    ada_dma(l + 1, 0, wab2[0])
                    ada_j = [0]
                for n in range(NTILE):
                    if n + 1 < NTILE:
                        gla_load(0, fwd[n + 1], n + 1)
                        gla_load(1, bwd[n + 1], n + 1)
                    gens = [gla_tile(0, fwd[n], n), gla_tile(1, bwd[n], n)]
                    while gens:
                        for g_ in list(gens):
                            try:
                                next(g_)
                            except StopIteration:
                                gens.remove(g_)
                    if do_ada and n % 3 != 2 and ada_j[0] < 12:
                        j = ada_j[0]
                        if j + 1 < 12:
                            ada_dma(l + 1, j + 1, wab2[(j + 1) % 2])
                        ada_block(l + 1, j, wab2[j % 2], [ps[0], ps[1], ps[4], ps[5]])
                        ada_j[0] = j + 1
                if do_ada:
                    assert ada_j[0] == 12
                    ada_finish(l + 1)
                S.barrier()

            wst = ExitStack()
            wbr = [S.tile(wst, "wbr%d" % i, [128, 8, D], BF16) for i in range(3)]
            wbr_q = [[S.res("wbrq%d_%d" % (br, q)) for q in range(4)] for br in range(3)]
            for q in range(4):
                for br in range(3):
                    S.dma("pool", wbr[br].t[:, :, q * 512:(q + 1) * 512],
                          wb_d[br][l, :, q * 512:(q + 1) * 512].rearrange("(k p) n -> p k n", p=128),
                          wbr_q[br][q], writes=[wbr_q[br][q]])
            with ExitStack() as st2:
                PD = 3
                oa = [S.tile(st2, "oa%d" % i, [128, 1024], F32) for i in range(PD)]
                obb = [S.tile(st2, "obb%d" % i, [128, 1024], F32) for i in range(PD)]
                sgg = [S.tile(st2, "sgg%d" % i, [128, 1024], BF16) for i in range(PD)]
                sqs_ = [S.tile(st2, "gsq%d" % i, [128, 1024], F32) for i in range(PD)]
                gob = [S.tile(st2, "gob%d" % i, [128, 1024], BF16) for i in range(PD)]
                s8 = [S.tile(st2, "gs8%d" % i, [128, 16], F32) for i in range(PD)]
                goT = [S.tile(st2, "goT%d" % i, [128, 8, 128], BF16) for i in range(PD)]
                v4 = lambda ap: ap.rearrange("p (g d) -> p g d", g=4)
                gn = rowp.t[:, RP + 256:RP + 512]
                def fp_tile(i):
                    b = i % PD
                    oa_, ob_, sg_, go_, s8_, gT_, sq_ = oa[b], obb[b], sgg[b], gob[b], s8[b], goT[b], sqs_[b]
                    S.dma("sp", oa_.t[:], O_d[0][i * 128:(i + 1) * 128, :], oa_.res, reads=[dres["OF"]], writes=[oa_.res])
                    S.dma("sp", ob_.t[:], O_d[1][i * 128:(i + 1) * 128, :], ob_.res, reads=[dres["OB"]], writes=[ob_.res])
                    S.dma("sp", sg_.t[:], SGG_d[i * 128:(i + 1) * 128, :], sg_.res, reads=[dres["SGG"]], writes=[sg_.res])
                    S.op("pool", lambda e: e.tensor_tensor(out=oa_.t[:], in0=oa_.t[:], in1=ob_.t[:], op=ALU.add),
                         reads=[oa_.res, ob_.res], writes=[oa_.res])
                    yield
                    S.op("act", lambda e: e.activation(out=sq_.t[:], in_=oa_.t[:], func=AF.Square), reads=[oa_.res], writes=[sq_.res])
                    yield
                    S.op("dve", lambda e: e.reduce_sum(out=s8_.t[:, 0:4], in_=v4(sq_.t[:]), axis=AX.X), reads=[sq_.res], writes=[s8_.res])
                    S.op("dve", lambda e: e.tensor_scalar(out=s8_.t[:, 0:4], in0=s8_.t[:, 0:4], scalar1=1.0 / 256, scalar2=EPS,
                                                          op0=ALU.mult, op1=ALU.add), reads=[s8_.res], writes=[s8_.res])
                    yield
                    S.op("act", lambda e: e.activation(out=s8_.t[:, 4:8], in_=s8_.t[:, 0:4], func=AF.Sqrt), reads=[s8_.res], writes=[s8_.res])
                    yield
                    S.op("dve", lambda e: e.reciprocal(out=s8_.t[:, 8:12], in_=s8_.t[:, 4:8]), reads=[s8_.res], writes=[s8_.res])
                    S.op("dve", lambda e: e.tensor_tensor(out=v4(oa_.t[:]), in0=v4(oa_.t[:]),
                                                          in1=s8_.t[:, 8:12].unsqueeze(2).to_broadcast([128, 4, 256]), op=ALU.mult),
                         reads=[oa_.res, s8_.res], writes=[oa_.res])
                    yield
                    S.op("pool", lambda e: e.tensor_tensor(out=v4(oa_.t[:]), in0=v4(oa_.t[:]),
                                                           in1=gn.unsqueeze(1).to_broadcast([128, 4, 256]), op=ALU.mult),
                         reads=[oa_.res, rowp.res], writes=[oa_.res])
                    yield
                    S.op("dve", lambda e: e.tensor_tensor(out=go_.t[:], in0=oa_.t[:], in1=sg_.t[:], op=ALU.mult),
                         reads=[oa_.res, sg_.res], writes=[go_.res])
                    yield
                    p2 = next_ps()
                    pb = p2.t[:].bitcast(BF16)
                    for cb in range(8):
                        S.op("pe", lambda e: e.transpose(out=pb[:, cb * 128:(cb + 1) * 128], in_=go_.t[:, cb * 128:(cb + 1) * 128],
                                                         identity=identb.t[:]),
                             reads=[go_.res, identb.res], writes=[p2.res], inc=(cb == 7))
                    yield
                    S.op("act", lambda e: e.activation(out=gT_.t[:].rearrange("p a b -> p (a b)"), in_=pb[:, 0:1024], func=AF.Copy),
                         reads=[p2.res], writes=[gT_.res])
                    S.dma("sp", GOT_d[:, :, i * 128:(i + 1) * 128].rearrange("c p t -> p c t"), gT_.t[:], gT_.res,
                          reads=[gT_.res], dwrites=[dres["GOT"]])

                active = []
                tiles_fp = list(range(0 if need_ctx else 2, NTILE))
                nxt = 0
                rnd = 0
                while nxt < len(tiles_fp) or active:
                    if nxt < len(tiles_fp) and len(active) < PD and rnd % 3 == 0:
                        active.append(fp_tile(tiles_fp[nxt]))
                        nxt += 1
                    rnd += 1
                    for g_ in list(active):
                        try:
                            next(g_)
                        except StopIteration:
                            active.remove(g_)
                S.barrier()
            if stop_after == "F":
                wst.close()
                return nc

            groups = TOK_GROUPS if need_ctx else [(256, 512), (768, 512), (1280, 512), (1792, 512)]
            with ExitStack() as st:
                xT = [[S.tile(st, "xT%d_%d" % (br, i), [128, 8, 512], BF16) for i in range(2)] for br in range(3)]
                mgt = [S.tile(st, "mgt%d" % i, [128, 3, 512], BF16) for i in range(3)]
                tm = [[S.tile(st, "tm%d_%d" % (br, i), [128, 512], F32) for i in range(2)] for br in range(3)]
                ys = [S.tile(st, "ys%d" % i, [128, 512], BF16) for i in range(3)]
                MG3 = MGT_d.rearrange("(b f) p t -> f p b t", b=3)
                srcs = (POT_d, DOT_d, GOT_d)
                names = ("POT", "DOT", "GOT")

                def load_x(gi):
                    t0, tsz = groups[gi]
                    for br in range(3):
                        S.dma("sp", xT[br][gi % 2].t[:, :, 0:tsz], srcs[br][:, :, t0:t0 + tsz].rearrange("c p t -> p c t"), xT[br][gi % 2].res,
                              reads=[dres[names[br]]], writes=[xT[br][gi % 2].res])
                flat = [(gi, fb) for gi in range(len(groups)) for fb in range(16)]

                def load_mg(n):
                    gi, fb = flat[n]
                    t0, tsz = groups[gi]
                    S.dma("sp", mgt[n % 3].t[:, :, 0:tsz], MG3[fb, :, :, t0:t0 + tsz], mgt[n % 3].res, reads=[dres["MGT"]], writes=[mgt[n % 3].res])
                load_x(0)
                load_mg(0)
                load_mg(1)
                for n, (gi, fb) in enumerate(flat):
                    t0, tsz = groups[gi]
                    gb_i = gi % 2
                    if fb == 0 and gi + 1 < len(groups):
                        load_x(gi + 1)
                    if n + 2 < len(flat):
                        load_mg(n + 2)
                    mg_ = mgt[n % 3]
                    ys_ = ys[n % 3]
                    pts = []
                    for br in range(3):
                        pt = next_ps()
                        pts.append(pt)
                        for k in range(8):
                            S.op("pe", lambda e: e.matmul(pt.t[:, 0:tsz], lhsT=wbr[br].t[:, k, fb * 128:(fb + 1) * 128],
                                                          rhs=xT[br][gb_i].t[:, k, 0:tsz], start=(k == 0), stop=(k == 7)),
                                 reads=[wbr_q[br][fb // 4], xT[br][gb_i].res], writes=[pt.res], inc=(k == 7))
                    tms = [tm[br][n % 2] for br in range(3)]
                    for br in range(3):
                        S.op("dve", lambda e: e.tensor_tensor(out=tms[br].t[:, 0:tsz], in0=pts[br].t[:, 0:tsz], in1=mg_.t[:, br, 0:tsz], op=ALU.mult),
                             reads=[pts[br].res, mg_.res], writes=[tms[br].res])
                    S.op("pool", lambda e: e.tensor_tensor(out=tms[0].t[:, 0:tsz], in0=tms[0].t[:, 0:tsz], in1=tms[1].t[:, 0:tsz], op=ALU.add),
                         reads=[tms[0].res, tms[1].res], writes=[tms[0].res])
                    S.op("pool", lambda e: e.tensor_tensor(out=ys_.t[:, 0:tsz], in0=tms[0].t[:, 0:tsz], in1=tms[2].t[:, 0:tsz], op=ALU.add),
                         reads=[tms[0].res, tms[2].res], writes=[ys_.res])
                    S.dma("sp", YT_d[fb, :, t0:t0 + tsz], ys_.t[:, 0:tsz], ys_.res, reads=[ys_.res], dwrites=[dres["YT"]])
                S.barrier()
            wst.close()
            if stop_after == "G1":
                return nc

            with ExitStack() as st:
                wo = S.tile(st, "wo", [128, 16, D], BF16)
                gb = [S.tile(st, "gb%d" % r, [128, D], F32) for r in range(2)]
                dg_ = S.tile(st, "dgm", [128, 128], F32)
                wo_q = [S.res("woq%d" % q) for q in range(4)]
                for hf in range(4):
                    S.dma("pool", wo.t[:, :, hf * 512:(hf + 1) * 512], w_out[l, :, hf * 512:(hf + 1) * 512].rearrange("(k p) n -> p k n", p=128),
                          wo_q[hf], writes=[wo_q[hf]])
                for r in range(2):
                    for k in range(16):
                        S.op("dve", lambda e: e.tensor_scalar(out=dg_.t[:], in0=ident.t[:], scalar1=mod.t[:, l * 48 + 32 + k, r:r + 1], scalar2=None,
                                                              op0=ALU.mult), reads=[ident.res, mod.res], writes=[dg_.res])
                        pt = next_ps()
                        S.op("pe", lambda e: e.matmul(pt.t[:, 0:128], lhsT=ones_f.t[:], rhs=dg_.t[:], start=True, stop=True),
                             reads=[ones_f.res, dg_.res], writes=[pt.res])
                        S.op("act", lambda e: e.activation(out=gb[r].t[:, k * 128:(k + 1) * 128], in_=pt.t[:, 0:128], func=AF.Copy),
                             reads=[pt.res], dwrites=[gb[r].res])
                yT = [S.tile(st, "yT%d" % i, [128, 16, 512], BF16) for i in range(2)]
                xr = [S.tile(st, "xr%d" % i, [128, D], F32) for i in range(2)]
                xo = [S.tile(st, "xo%d" % i, [128, D], F32) for i in range(2)]
                tg_ = [S.tile(st, "tgm%d" % i, [128, 512], F32) for i in range(2)]
                tc = 0
                tiles_g2 = [(gi, ti) for gi, (t0, tsz) in enumerate(groups) for ti in range(tsz // 128)]

                def load_y(gi):
                    t0, tsz = groups[gi]
                    S.dma("sp", yT[gi % 2].t[:, :, 0:tsz], YT_d[:, :, t0:t0 + tsz].rearrange("c p t -> p c t"), yT[gi % 2].res, reads=[dres["YT"]], writes=[yT[gi % 2].res])

                def load_xr(m):
                    gi, ti = tiles_g2[m]
                    i = groups[gi][0] // 128 + ti
                    S.dma("sp", xr[i % 2].t[:], xsrc[i * 128:(i + 1) * 128, :], xr[i % 2].res, reads=[dres["X1"]] if l > 0 else [], writes=[xr[i % 2].res])
                load_y(0)
                load_xr(0)
                for m, (gi, ti) in enumerate(tiles_g2):
                    t0, tsz = groups[gi]
                    y_ = yT[gi % 2]
                    if ti == 0 and gi + 1 < len(groups):
                        load_y(gi + 1)
                    if m + 1 < len(tiles_g2):
                        load_xr(m + 1)
                    if True:
                        i = t0 // 128 + ti
                        r = 1 if i < 2 else 0
                        xr_, xo_ = xr[i % 2], xo[i % 2]
                        for nb in range(4):
                            pt = next_ps()
                            for k in range(16):
                                S.op("pe", lambda e: e.matmul(pt.t[:, :], lhsT=y_.t[:, k, ti * 128:(ti + 1) * 128], rhs=wo.t[:, k, nb * 512:(nb + 1) * 512],
                                                              start=(k == 0), stop=(k == 15)),
                                     reads=[y_.res, wo_q[nb]], writes=[pt.res], inc=(k == 15))
                            tc += 1
                            tgx = tg_[tc % 2]
                            S.op("dve", lambda e: e.tensor_tensor(out=tgx.t[:], in0=pt.t[:], in1=gb[r].t[:, nb * 512:(nb + 1) * 512], op=ALU.mult),
                                 reads=[pt.res, gb[r].res], writes=[tgx.res])
                            S.op("pool", lambda e: e.tensor_tensor(out=xo_.t[:, nb * 512:(nb + 1) * 512], in0=tgx.t[:], in1=xr_.t[:, nb * 512:(nb + 1) * 512],
                                                                   op=ALU.add),
                                 reads=[tgx.res, xr_.res], writes=[xo_.res])
                        if l == n_layers - 1:
                            S.dma("sp", y_out[(i - 2) * 128:(i - 1) * 128, :], xo_.t[:], xo_.res, reads=[xo_.res], dwrites=[dres["Y"]])
                        else:
                            S.dma("sp", X1_d[i * 128:(i + 1) * 128, :], xo_.t[:], xo_.res, reads=[xo_.res], dwrites=[dres["X1"]])
                S.barrier()
        S.barrier()
        print("program built: inst=%d waits=%d sems=%d" % (S.ninst, S.nwaits, S.nsem))
    return nc


_CONSTS = None


def _consts():
    global _CONSTS
    if _CONSTS is None:
        _CONSTS = {
            "ident": np.eye(128, dtype=np.float32),
            "band": _band_tables(),
            "rope": _rope_tables(),
            "tri": _tri_tables(),
            "tri2": _tri2_tables(),
        }
    return _CONSTS


def _host_inputs(inputs, b):
    f = lambda a: np.ascontiguousarray(np.asarray(a, dtype=np.float32))
    x, ctx, c, c_ctx = f(inputs["x"]), f(inputs["ctx"]), f(inputs["c"]), f(inputs["c_ctx"])
    xin = np.concatenate([ctx[b], x[b]], axis=0)
    cT = np.stack([c[b].reshape(16, 128).T, c_ctx.reshape(16, 128).T], axis=-1).reshape(128, 32)
    colp = np.zeros((128, 2, 72), np.float32)
    rowp = np.zeros((2, 768), np.float32)
    w2 = np.zeros((2, 2, 17, 512), np.float32)
    for l in range(2):
        colp[:, l, 0:16] = f(inputs["norm_g"])[l].reshape(16, 128).T
        colp[:, l, 16:64] = f(inputs["b_ada"])[l].reshape(48, 128).T
        colp[:, l, 64:72] = f(inputs["pool_scale"])[l].reshape(8, 128).T
        rowp[l, 0:64] = f(inputs["diff_q_norm"])[l]
        rowp[l, 64:128] = f(inputs["diff_k_norm"])[l]
        rowp[l, 128:256] = f(inputs["diff_subln"])[l]
        rowp[l, 256:512] = f(inputs["gla_norm"])[l]
        rowp[l, 512:576] = f(inputs["diff_lam_q1"])[l]
        rowp[l, 576:640] = f(inputs["diff_lam_k1"])[l]
        rowp[l, 640:704] = f(inputs["diff_lam_q2"])[l]
        rowp[l, 704:768] = f(inputs["diff_lam_k2"])[l]
        w2[l, 0, 0:16] = f(inputs["gla_w_gate_f"])[l]
        w2[l, 0, 16] = f(inputs["gla_b_gate_f"])[l]
        w2[l, 1, 0:16] = f(inputs["gla_w_gate_b"])[l]
        w2[l, 1, 16] = f(inputs["gla_b_gate_b"])[l]
    m = {
        "xin": np.ascontiguousarray(xin), "cT": np.ascontiguousarray(cT),
        "w_ada": f(inputs["w_ada"]), "w_in": f(inputs["w_in"]), "pool_w": f(inputs["pool_w"]),
        "w_branch_pool": f(inputs["w_branch_pool"]), "w_branch_diff": f(inputs["w_branch_diff"]),
        "w_branch_gla": f(inputs["w_branch_gla"]), "w_out": f(inputs["w_out"]),
        "colp": np.ascontiguousarray(colp.reshape(128, 144)), "rowp": rowp, "w2": w2,
    }
    m.update(_consts())
    return m


def kernel(**inputs):
    nc = build_program()
    in_maps = [_host_inputs(inputs, b) for b in range(N_ACTIVE)]
    res = run_bass_kernel_spmd(nc, in_maps, core_ids=list(range(N_ACTIVE)))
    out = np.stack([np.asarray(res.results[b]["y"], dtype=np.float32) for b in range(N_ACTIVE)], axis=0)
    return out
```
